# Optimizing a Trainium2 kernel written in Bass

```python
import math
import jax, jax.numpy as jnp
from jax import lax
import numpy as np

D_MODEL = 2048
BATCH = 2
SEQ = 4096
DEPTH = 1

CHUNK = 64
N_META = 16
D_MIX = D_MODEL
ATTN_WIDTH = D_MIX // 2
N_DIFF_HEADS = 8
DIFF_HEAD_DIM = ATTN_WIDTH // (2 * N_DIFF_HEADS)
V_HEAD_DIM = 2 * DIFF_HEAD_DIM
ROT_DIM = DIFF_HEAD_DIM // 4
ROPE_THETA = 500000.0
Q_BLOCK = 128
SSD_WIDTH = D_MIX - ATTN_WIDTH
SSD_HEAD_DIM = 64
N_SSD_HEADS = SSD_WIDTH // SSD_HEAD_DIM
N_SSD_GROUPS = 2
HEADS_PER_GROUP = N_SSD_HEADS // N_SSD_GROUPS
D_STATE = 128
CONV_WIDTH = 4
CONV_DIM = SSD_WIDTH + 2 * N_SSD_GROUPS * D_STATE
D_FF = 5632
EPS = 1e-6
IN_PROJ_DIM = 3 * ATTN_WIDTH + SSD_WIDTH + CONV_DIM + N_SSD_HEADS

kernel_name = "hymba_diffattn_ssd_macaron_block"


def _rmsnorm(x, w):
    x32 = x.astype(jnp.float32)
    y = x32 * lax.rsqrt(jnp.mean(x32 * x32, axis=-1, keepdims=True) + EPS)
    return (y * w.astype(jnp.float32)).astype(x.dtype)


def _swiglu(h, w_gate, w_up, w_down):
    return (jax.nn.silu(h @ w_gate) * (h @ w_up)) @ w_down


def _chunk_ids(n_pos):
    p = jnp.arange(n_pos, dtype=jnp.int32)
    return jnp.where(p < N_META, 0, (p - N_META) // CHUNK + 1)


def _rope_tables(n_pos):
    inv = jnp.power(ROPE_THETA, -jnp.arange(0, ROT_DIM, 2, dtype=jnp.float32) / ROT_DIM)
    ang = jnp.arange(n_pos, dtype=jnp.float32)[:, None] * inv[None, :]
    return jnp.cos(ang), jnp.sin(ang)


def _partial_rope(x, cos, sin):
    c = cos[None, :, None, None, :].astype(x.dtype)
    s = sin[None, :, None, None, :].astype(x.dtype)
    half = ROT_DIM // 2
    x1 = x[..., :half]
    x2 = x[..., half:ROT_DIM]
    return jnp.concatenate([x1 * c - x2 * s, x2 * c + x1 * s, x[..., ROT_DIM:]], axis=-1)


def _diff_attend(q_blk, q_cid, k, v, k_cid, lam):
    s = jnp.einsum('bqhsd,bkhsd->bhsqk', q_blk, k,
                   preferred_element_type=jnp.float32) * (DIFF_HEAD_DIM ** -0.5)
    mask = k_cid[None, :] <= q_cid[:, None]
    s = jnp.where(mask, s, -jnp.inf)
    p = jax.nn.softmax(s, axis=-1)
    w = p[:, :, 0] - lam * p[:, :, 1]
    return jnp.einsum('bhqk,bkhv->bqhv', w.astype(v.dtype), v)


def _diff_attention(q, k, v, cid, lam):
    bsz, n_pos = q.shape[0], q.shape[1]
    n_real = n_pos - N_META
    n_blocks = n_real // Q_BLOCK
    out_meta = _diff_attend(q[:, :N_META], cid[:N_META], k, v, cid, lam)
    qb = jnp.moveaxis(q[:, N_META:].reshape(bsz, n_blocks, Q_BLOCK, N_DIFF_HEADS, 2, DIFF_HEAD_DIM), 1, 0)
    cb = cid[N_META:].reshape(n_blocks, Q_BLOCK)
    out_real = lax.map(lambda a: _diff_attend(a[0], a[1], k, v, cid, lam), (qb, cb))
    out_real = jnp.moveaxis(out_real, 0, 1).reshape(bsz, n_real, N_DIFF_HEADS, V_HEAD_DIM)
    return jnp.concatenate([out_meta, out_real], axis=1)


def _ssd_mixer(z, xbc, dt_raw, conv_w, conv_b, dt_bias, a_log, d_skip, norm_w):
    bsz, n_pos, _ = xbc.shape
    G, E, P, N = N_SSD_GROUPS, HEADS_PER_GROUP, SSD_HEAD_DIM, D_STATE
    xbc = lax.conv_general_dilated(xbc, conv_w[:, None, :], window_strides=(1,),
                                   padding=[(CONV_WIDTH - 1, 0)],
                                   dimension_numbers=('NWC', 'WIO', 'NWC'),
                                   feature_group_count=CONV_DIM)
    xbc = jax.nn.silu(xbc + conv_b)
    x_s = xbc[..., :SSD_WIDTH]
    b_s = xbc[..., SSD_WIDTH:SSD_WIDTH + G * N]
    c_s = xbc[..., SSD_WIDTH + G * N:]
    dt = jax.nn.softplus(dt_raw.astype(jnp.float32) + dt_bias.astype(jnp.float32))
    pad = (-n_pos) % CHUNK
    def padl(t):
        return jnp.pad(t, [(0, 0), (pad, 0)] + [(0, 0)] * (t.ndim - 2))
    x_s, b_s, c_s, dt = padl(x_s), padl(b_s), padl(c_s), padl(dt)
    n_chunks = (n_pos + pad) // CHUNK
    X = x_s.reshape(bsz, n_chunks, CHUNK, G, E, P)
    Bm = b_s.reshape(bsz, n_chunks, CHUNK, G, N)
    Cm = c_s.reshape(bsz, n_chunks, CHUNK, G, N)
    dt = dt.reshape(bsz, n_chunks, CHUNK, G, E)
    A = -jnp.exp(a_log.astype(jnp.float32)).reshape(G, E)
    dA = dt * A
    Xdt = X * dt[..., None].astype(X.dtype)
    a_cs = jnp.cumsum(dA, axis=2)
    tril = jnp.tril(jnp.ones((CHUNK, CHUNK), dtype=bool))
    seg = a_cs[:, :, :, None] - a_cs[:, :, None, :]
    decay_in = jnp.exp(jnp.where(tril[None, None, :, :, None, None], seg, -jnp.inf))
    cb = jnp.einsum('bclgn,bcsgn->bclsg', Cm, Bm)
    y_diag = jnp.einsum('bclsg,bclsge,bcsgep->bclgep', cb, decay_in.astype(cb.dtype), Xdt)
    decay_to_end = jnp.exp(a_cs[:, :, -1:] - a_cs)
    chunk_states = jnp.einsum('bclgn,bclge,bclgep->bcgepn', Bm, decay_to_end.astype(Bm.dtype), Xdt)
    chunk_decay = jnp.exp(a_cs[:, :, -1])

    def step(state, inp):
        cs, dec = inp
        return state * dec[..., None, None] + cs, state

    init = jnp.zeros((bsz, G, E, P, N), jnp.float32)
    _, states_in = lax.scan(step, init, (jnp.moveaxis(chunk_states, 1, 0), jnp.moveaxis(chunk_decay, 1, 0)))
    states_in = jnp.moveaxis(states_in, 0, 1).astype(Cm.dtype)
    y_off = jnp.einsum('bclgn,bcgepn,bclge->bclgep', Cm, states_in, jnp.exp(a_cs).astype(Cm.dtype))
    y = y_diag + y_off + X * d_skip.reshape(G, E)[:, :, None].astype(X.dtype)
    y = y.reshape(bsz, n_pos + pad, SSD_WIDTH)[:, pad:].astype(z.dtype)
    return _rmsnorm(y * jax.nn.silu(z), norm_w)


def setup_inputs(seed: int = 0) -> dict:
    key = jax.random.key(seed)
    ks = jax.random.split(key, 32)
    f32 = jnp.float32

    def nrm(k, shape, scale):
        return jax.random.normal(k, shape, f32) * scale

    def gain(k, dim):
        return 1.0 + 0.01 * jax.random.normal(k, (DEPTH, dim), f32)

    dt0 = jnp.exp(jax.random.uniform(ks[20], (DEPTH, N_SSD_HEADS), f32,
                                     minval=math.log(1e-3), maxval=math.log(1e-1)))
    return {
        "x": nrm(ks[0], (BATCH, SEQ, D_MODEL), 1.0),
        "meta_tokens": nrm(ks[1], (N_META, D_MODEL), 1.0),
        "ffn1_norm": gain(ks[2], D_MODEL),
        "ffn1_w_gate": nrm(ks[3], (DEPTH, D_MODEL, D_FF), D_MODEL ** -0.5),
        "ffn1_w_up": nrm(ks[4], (DEPTH, D_MODEL, D_FF), D_MODEL ** -0.5),
        "ffn1_w_down": nrm(ks[5], (DEPTH, D_FF, D_MODEL), D_FF ** -0.5),
        "mix_norm": gain(ks[6], D_MODEL),
        "w_in": nrm(ks[7], (DEPTH, D_MODEL, IN_PROJ_DIM), D_MODEL ** -0.5),
        "q_norm": gain(ks[8], DIFF_HEAD_DIM),
        "k_norm": gain(ks[9], DIFF_HEAD_DIM),
        "lambda_q1": nrm(ks[10], (DEPTH, DIFF_HEAD_DIM), 0.1),
        "lambda_k1": nrm(ks[11], (DEPTH, DIFF_HEAD_DIM), 0.1),
        "lambda_q2": nrm(ks[12], (DEPTH, DIFF_HEAD_DIM), 0.1),
        "lambda_k2": nrm(ks[13], (DEPTH, DIFF_HEAD_DIM), 0.1),
        "attn_out_norm": gain(ks[14], V_HEAD_DIM),
        "conv_w": nrm(ks[15], (DEPTH, CONV_WIDTH, CONV_DIM), CONV_WIDTH ** -0.5),
        "conv_b": nrm(ks[16], (DEPTH, CONV_DIM), 0.01),
        "dt_bias": dt0 + jnp.log(-jnp.expm1(-dt0)),
        "a_log": jnp.log(jax.random.uniform(ks[17], (DEPTH, N_SSD_HEADS), f32, minval=1.0, maxval=16.0)),
        "d_skip": 1.0 + 0.01 * jax.random.normal(ks[18], (DEPTH, N_SSD_HEADS), f32),
        "ssd_norm": gain(ks[19], SSD_WIDTH),
        "w_out": nrm(ks[21], (DEPTH, D_MIX, D_MODEL), D_MIX ** -0.5),
        "ffn2_norm": gain(ks[22], D_MODEL),
        "ffn2_w_gate": nrm(ks[23], (DEPTH, D_MODEL, D_FF), D_MODEL ** -0.5),
        "ffn2_w_up": nrm(ks[24], (DEPTH, D_MODEL, D_FF), D_MODEL ** -0.5),
        "ffn2_w_down": nrm(ks[25], (DEPTH, D_FF, D_MODEL), D_FF ** -0.5),
    }


def reference(x, meta_tokens, ffn1_norm, ffn1_w_gate, ffn1_w_up, ffn1_w_down, mix_norm, w_in,
              q_norm, k_norm, lambda_q1, lambda_k1, lambda_q2, lambda_k2, attn_out_norm,
              conv_w, conv_b, dt_bias, a_log, d_skip, ssd_norm, w_out,
              ffn2_norm, ffn2_w_gate, ffn2_w_up, ffn2_w_down):
    bsz = x.shape[0]
    meta = jnp.broadcast_to(meta_tokens.astype(x.dtype)[None], (bsz, N_META, D_MODEL))
    h = jnp.concatenate([meta, x], axis=1)
    n_pos = h.shape[1]
    cid = _chunk_ids(n_pos)
    cos, sin = _rope_tables(n_pos)

    for l in range(DEPTH):
        h = h + 0.5 * _swiglu(_rmsnorm(h, ffn1_norm[l]), ffn1_w_gate[l], ffn1_w_up[l], ffn1_w_down[l])

        u = _rmsnorm(h, mix_norm[l]) @ w_in[l]
        o = 0
        q = u[..., o:o + ATTN_WIDTH]; o += ATTN_WIDTH
        k = u[..., o:o + ATTN_WIDTH]; o += ATTN_WIDTH
        v = u[..., o:o + ATTN_WIDTH]; o += ATTN_WIDTH
        z = u[..., o:o + SSD_WIDTH]; o += SSD_WIDTH
        xbc = u[..., o:o + CONV_DIM]; o += CONV_DIM
        dt_raw = u[..., o:o + N_SSD_HEADS]

        q = q.reshape(bsz, n_pos, N_DIFF_HEADS, 2, DIFF_HEAD_DIM)
        k = k.reshape(bsz, n_pos, N_DIFF_HEADS, 2, DIFF_HEAD_DIM)
        v = v.reshape(bsz, n_pos, N_DIFF_HEADS, V_HEAD_DIM)
        q = _partial_rope(_rmsnorm(q, q_norm[l]), cos, sin)
        k = _partial_rope(_rmsnorm(k, k_norm[l]), cos, sin)
        lam_init = 0.8 - 0.6 * math.exp(-0.3 * l)
        lam = (jnp.exp(jnp.sum(lambda_q1[l].astype(jnp.float32) * lambda_k1[l].astype(jnp.float32)))
               - jnp.exp(jnp.sum(lambda_q2[l].astype(jnp.float32) * lambda_k2[l].astype(jnp.float32)))
               + lam_init)
        attn = _diff_attention(q, k, v, cid, lam)
        attn = (_rmsnorm(attn, attn_out_norm[l]) * (1.0 - lam_init)).reshape(bsz, n_pos, ATTN_WIDTH)

        ssd = _ssd_mixer(z, xbc, dt_raw, conv_w[l], conv_b[l], dt_bias[l], a_log[l], d_skip[l], ssd_norm[l])

        h = h + jnp.concatenate([attn, ssd.astype(attn.dtype)], axis=-1) @ w_out[l]

        h = h + 0.5 * _swiglu(_rmsnorm(h, ffn2_norm[l]), ffn2_w_gate[l], ffn2_w_up[l], ffn2_w_down[l])

    return h[:, N_META:]
```

```python
import contextlib
import math
import numpy as np
import ml_dtypes
import concourse.bass as bass
import concourse.mybir as mybir
from concourse.bass_utils import run_bass_kernel_spmd

F32 = mybir.dt.float32
BF16 = mybir.dt.bfloat16
ALU = mybir.AluOpType
AF = mybir.ActivationFunctionType
NPBF = ml_dtypes.bfloat16

D = 2048
DC = 16
DFF = 5632
FC = 44
NQR = 4
FQ = 11
NR = 1024
NT = 1048
META0 = 1024
HALO0 = 1040
TCH = [(0, 512), (512, 512), (1024, 24)]
EPS = 1e-6
NEG = -30000.0
GROUPS = [[0, 1, 2, 3], [4, 5, 6, 7]]
USE_CC = True

PK = {}
_o = 0
for _n, _w in [("g1", 16), ("gm", 16), ("g2", 16), ("gssd", 8), ("gattn", 1), ("gq", 1), ("gk", 1),
               ("cw", 48), ("cb", 12), ("dsk", 8), ("dtb", 16), ("alog", 16), ("lam", 4),
               ("bm", 20), ("msk", 5)]:
    PK[_n] = (_o, _w)
    _o += _w
NPK = _o


class Buf:
    __slots__ = ("name", "lw", "rd", "sem", "semval")

    def __init__(self, name):
        self.name = name
        self.lw = None
        self.rd = {}
        self.sem = None
        self.semval = 0


class Sched:
    ENG = ("pe", "act", "dve", "pool", "sp")

    def __init__(self, nc, stack):
        self.nc = nc
        self.stack = stack
        self.eng = {"pe": nc.tensor, "act": nc.scalar, "dve": nc.vector, "pool": nc.gpsimd, "sp": nc.sync}
        self.sem = {e: stack.enter_context(nc.semaphore(f"s_{e}")) for e in self.ENG}
        self.n = {e: 0 for e in self.ENG}
        self.last = {e: None for e in self.ENG}
        self.evs = {e: [] for e in self.ENG}
        self.cnt = {e: 0 for e in self.ENG}
        self.waited = {e: {} for e in self.ENG}
        self.dbufs = []

    def _event_for(self, e, seq):
        evs = self.evs[e]
        lo, hi = 0, len(evs)
        while lo < hi:
            mid = (lo + hi) // 2
            if evs[mid][0] >= seq:
                hi = mid
            else:
                lo = mid + 1
        if lo < len(evs):
            return (self.sem[e], evs[lo][1])
        self.cnt[e] += 1
        self.last[e].then_inc(self.sem[e], 1)
        evs.append((self.n[e] - 1, self.cnt[e]))
        return (self.sem[e], self.cnt[e])

    def _wait(self, e, tok):
        if tok is None:
            return
        if tok[0] == e and e == "pe":
            return
        if tok[0] == "dma":
            sem, val = tok[1], tok[2]
        else:
            sem, val = self._event_for(tok[0], tok[1])
        key = id(sem)
        w = self.waited[e]
        if w.get(key, 0) >= val:
            return
        w[key] = val
        self.eng[e].wait_ge(sem, val)

    def _deps(self, e, reads, writes):
        for b in reads:
            self._wait(e, b.lw)
        for b in writes:
            self._wait(e, b.lw)
            for t in b.rd.values():
                self._wait(e, t)

    @staticmethod
    def _mark(tok, key, reads, writes):
        for b in reads:
            b.rd[key] = tok
        for b in writes:
            b.lw = tok
            b.rd = {}

    def op(self, e, fn, reads=(), writes=()):
        self._deps(e, reads, writes)
        ins = fn(self.eng[e])
        seq = self.n[e]
        self.n[e] += 1
        self.last[e] = ins
        self._mark((e, seq), e, reads, writes)
        return ins

    def dmaish(self, q, fn, reads=(), writes=(), inc=16):
        self._deps(q, reads, writes)
        wb = writes[0]
        if wb.sem is None:
            wb.sem = self.stack.enter_context(self.nc.semaphore(f"d{len(self.dbufs)}_{wb.name}"))
            self.dbufs.append(wb)
        wb.semval += inc
        ins = fn(self.eng[q])
        ins.then_inc(wb.sem, inc)
        tok = ("dma", wb.sem, wb.semval)
        self._mark(tok, id(wb.sem), reads, writes)
        return ins

    def dma(self, q, out, in_, reads=(), writes=()):
        return self.dmaish(q, lambda e: e.dma_start(out=out, in_=in_), reads, writes)

    def barrier(self):
        toks = [(e2, self.n[e2] - 1) for e2 in self.ENG if self.n[e2] > 0]
        toks += [("dma", b.sem, b.semval) for b in self.dbufs]
        for e in self.ENG:
            for t in toks:
                if t[0] != e:
                    self._wait(e, t)

    def wait_all(self, e, bufs):
        for b in bufs:
            self._wait(e, b.lw)


def build_program(stop_after=None, dbg=False, skip_ffn1=False):
    nc = bass.Bass("TRN2", target_bir_lowering=False)

    def din(name, shape, dt=F32):
        return nc.dram_tensor(name, list(shape), dt, kind="ExternalInput").ap()

    def dscr(name, shape, dt=F32):
        return nc.dram_tensor(name, list(shape), dt).ap()

    xT = din("xT", [128, DC, NT])
    wg = [din("wg1", [FC, 128, DC, 128]), din("wg2", [FC, 128, DC, 128])]
    wu = [din("wu1", [FC, 128, DC, 128]), din("wu2", [FC, 128, DC, 128])]
    wd = [din("wd1", [NQR, DC, 128, FQ, 128]), din("wd2", [NQR, DC, 128, FQ, 128])]
    winfm = din("winfm", [36, 128, DC, 128])
    winv = din("winv", [128, DC, 1024])
    windt = din("windt", [128, DC, 16])
    wout = din("wout", [DC, 128, DC, 128])
    pk_d = din("pk", [128, NPK])
    cos_d = din("cosF", [128, NT])
    sin_d = din("sinF", [128, NT])
    wmask_d = din("wmask", [64, NR], BF16)
    umask_d = din("umask", [64, 4 * NR], BF16)
    cbf_d = din("cbf", [128, 4, 128], BF16)
    cf_d = din("cf", [128, 4, 128])
    outT = nc.dram_tensor("outT", [128, DC, NR], F32, kind="ExternalOutput").ap()
    dbg_d = nc.dram_tensor("dbg", [128, DC, NT], F32, kind="ExternalOutput").ap() if dbg else None

    q_d = dscr("q_d", [8, 128, NR], BF16)
    ksend = [dscr(f"ksend{j}", [NR, 256], BF16) for j in range(4)]
    kall = [dscr(f"kall{j}", [4 * NR, 256], BF16) for j in range(4)]
    vsend = [dscr(f"vsend{j}", [256, NR], BF16) for j in range(4)]
    vall = [dscr(f"vall{j}", [4 * 256, NR], BF16) for j in range(4)]
    zs_d = dscr("zs_d", [8, 128, NR], BF16)
    xfm_d = dscr("xfm_d", [8, 128, NR], F32)
    cfm_d = dscr("cfm_d", [2, 128, NR + 16], BF16)
    bfm_d = dscr("bfm_d", [2, 128, NR + 16], BF16)
    xtok_d = dscr("xtok_d", [9, 128, 1024], F32)
    btok_d = dscr("btok_d", [9, 128, 256], BF16)
    fsend = [dscr(f"fsend{j}", [128, 512], F32) for j in range(2)]
    fall = [dscr(f"fall{j}", [4 * 128, 512], F32) for j in range(2)]
    dsend = dscr("dsend", [128, 16], F32)
    dall = dscr("dall", [4 * 128, 16], F32)

    st = contextlib.ExitStack()
    with st:
        S = Sched(nc, st)

        def sb(name, shape, dt=F32):
            return st.enter_context(nc.sbuf_tensor("s_" + name, list(shape), dt))

        PS = [st.enter_context(nc.psum_tensor(f"ps{i}", [128, 512], F32)) for i in range(8)]
        PB = [Buf(f"ps{i}") for i in range(8)]

        h = sb("h", [128, DC, NT])
        hB = Buf("h")
        pk = sb("pk", [128, NPK])
        cbf = sb("cbf", [128, 4, 128], BF16)
        cf = sb("cf", [128, 4, 128])
        cB = Buf("consts")
        S.dma("sp", pk[:], pk_d[:, :], writes=[cB])
        S.dma("sp", cbf[:], cbf_d[:, :, :], writes=[cB])
        S.dma("sp", cf[:], cf_d[:, :, :], writes=[cB])
        for c in range(DC):
            S.dma("sp" if c % 2 else "act", h[:, c, :], xT[:, c, :], writes=[hB])
        ident_b, ones_b, bd64_b, rot_b = (cbf[:, i, :] for i in range(4))
        ident_f, ones_f, tle_f, tgt_f = (cf[:, i, :] for i in range(4))

        def pkc(name, i=0, n=1):
            o, w = PK[name]
            return pk[:, o + i:o + i + n]

        def ffn(li, gname, tcs):
            with contextlib.ExitStack() as fst:
                def fsb(name, shape, dt=F32):
                    return fst.enter_context(nc.sbuf_tensor(f"s_{name}_{li}", list(shape), dt))
                nonlocal_sb = fsb
                hn = fsb("hn", [128, DC, NT], BF16)
                hnB = Buf("hn")
                act = fsb("act", [128, FQ, NT], BF16)
                actB = [Buf(f"act{i}") for i in range(FQ)]
                wgb = [fsb(f"wgb{i}", [128, DC, 128], BF16) for i in range(2)]
                wub = [fsb(f"wub{i}", [128, DC, 128], BF16) for i in range(2)]
                wgB = [Buf(f"wgb{i}") for i in range(2)]
                wuB = [Buf(f"wub{i}") for i in range(2)]
                wdb = [fsb(f"wdb{i}", [128, FQ, 128], BF16) for i in range(2)]
                wdB = [Buf(f"wdb{i}") for i in range(2)]
                sg = [fsb(f"sg{i}", [128, 512]) for i in range(2)]
                sgB = [Buf(f"sg{i}") for i in range(2)]
                sq = [fsb(f"sq{i}", [128, 512], BF16) for i in range(2)]
                sqB = [Buf(f"sq{i}") for i in range(2)]
                rt = fsb("rt", [128, 512]); rtB = Buf("rt")
                rs = fsb("rs", [128, 512]); rsB = Buf("rs")
                for (t0, tn) in tcs:
                    for c in range(DC):
                        S.op("act", lambda e, c=c: e.activation(out=sq[c % 2][:, 0:tn], in_=h[:, c, t0:t0 + tn],
                                                                func=AF.Square), reads=[hB], writes=[sqB[c % 2]])
                        S.op("pe", lambda e, c=c: e.matmul(PS[7][:, 0:tn], ones_b, sq[c % 2][:, 0:tn],
                                                           start=(c == 0), stop=(c == DC - 1)),
                             reads=[sqB[c % 2], cB], writes=[PB[7]])
                    S.op("act", lambda e: e.activation(out=rt[:, 0:tn], in_=PS[7][:, 0:tn], func=AF.Sqrt,
                                                       scale=1.0 / D, bias=EPS), reads=[PB[7]], writes=[rtB])
                    S.op("dve", lambda e: e.reciprocal(out=rs[:, 0:tn], in_=rt[:, 0:tn]), reads=[rtB], writes=[rsB])
                    for c in range(DC):
                        S.op("dve", lambda e, c=c: e.scalar_tensor_tensor(
                            out=hn[:, c, t0:t0 + tn], in0=h[:, c, t0:t0 + tn], scalar=pkc(gname, c),
                            in1=rs[:, 0:tn], op0=ALU.mult, op1=ALU.mult), reads=[hB, rsB, cB], writes=[hnB])
                it = 0
                jd = 0
                for qr in range(NQR):
                    for fi in range(FQ):
                        f = qr * FQ + fi
                        sl = f % 2
                        S.dma("pool", wgb[sl][:], wg[li][f], writes=[wgB[sl]])
                        S.dma("pool", wub[sl][:], wu[li][f], writes=[wuB[sl]])
                        for (t0, tn) in tcs:
                            bg, bu = it % 2, 2 + it % 2
                            for k in range(DC):
                                S.op("pe", lambda e, k=k: e.matmul(PS[bg][:, 0:tn], wgb[sl][:, k, :], hn[:, k, t0:t0 + tn],
                                                                   start=(k == 0), stop=(k == DC - 1)),
                                     reads=[wgB[sl], hnB], writes=[PB[bg]])
                            for k in range(DC):
                                S.op("pe", lambda e, k=k: e.matmul(PS[bu][:, 0:tn], wub[sl][:, k, :], hn[:, k, t0:t0 + tn],
                                                                   start=(k == 0), stop=(k == DC - 1)),
                                     reads=[wuB[sl], hnB], writes=[PB[bu]])
                            S.op("act", lambda e: e.activation(out=sg[it % 2][:, 0:tn], in_=PS[bg][:, 0:tn], func=AF.Silu),
                                 reads=[PB[bg]], writes=[sgB[it % 2]])
                            S.op("dve", lambda e: e.tensor_tensor(out=act[:, fi, t0:t0 + tn], in0=sg[it % 2][:, 0:tn],
                                                                  in1=PS[bu][:, 0:tn], op=ALU.mult),
                                 reads=[sgB[it % 2], PB[bu]], writes=[actB[fi]])
                            it += 1
                    for dc in range(DC):
                        sl = jd % 2
                        S.dma("pool", wdb[sl][:], wd[li][qr, dc], writes=[wdB[sl]])
                        for (t0, tn) in tcs:
                            bd = 4 + jd % 2
                            for fi in range(FQ):
                                S.op("pe", lambda e, fi=fi: e.matmul(PS[bd][:, 0:tn], wdb[sl][:, fi, :], act[:, fi, t0:t0 + tn],
                                                                     start=(fi == 0), stop=(fi == FQ - 1)),
                                     reads=[wdB[sl], actB[fi]], writes=[PB[bd]])
                            S.op("dve", lambda e: e.scalar_tensor_tensor(
                                out=h[:, dc, t0:t0 + tn], in0=PS[bd][:, 0:tn], scalar=0.5, in1=h[:, dc, t0:t0 + tn],
                                op0=ALU.mult, op1=ALU.add), reads=[PB[bd], hB], writes=[hB])
                            jd += 1
                S.barrier()

        dbg_outs = {}

        def dump(name, src_ap, shape, dt, readsB):
            if not dbg:
                return
            t = nc.dram_tensor("dbg_" + name, list(shape), dt, kind="ExternalOutput").ap()
            b = Buf("dbg_" + name)
            S.dma("sp", t, src_ap, reads=readsB, writes=[b])
            dbg_outs[name] = b

        def finish():
            oB = Buf("out")
            for c in range(DC):
                S.dma("sp", outT[:, c, :], h[:, c, 0:NR], reads=[hB], writes=[oB])
            S.wait_all("sp", [oB] + list(dbg_outs.values()))

        def bc(ap2, n):
            return ap2.unsqueeze(2).to_broadcast([ap2.shape[0], ap2.shape[1], n])

        def bcm(ap2, n):
            return ap2.unsqueeze(1).to_broadcast([ap2.shape[0], n, ap2.shape[1]])

        if not skip_ffn1:
            ffn(0, "g1", [(0, 350), (350, 350), (700, 348)])
        if stop_after == "ffn1":
            dump("h", h[:], [128, DC, NT], F32, [hB])
            finish()
            return nc

        mst = contextlib.ExitStack()
        with mst:
            def msb(name, shape, dt=F32):
                return mst.enter_context(nc.sbuf_tensor("m_" + name, list(shape), dt))

            dt_sb = msb("dt", [128, 9, 16]); dtB = Buf("dt")
            dA_sb = msb("dA", [128, 9, 16]); dAB = Buf("dA")
            kmeta = msb("kmeta", [128, 8, 2, 16], BF16); kmB = Buf("kmeta")
            vmeta = msb("vmeta", [16, 1024], BF16); vmB = Buf("vmeta")
            nlam = msb("nlam", [128, 1]); nlamB = Buf("nlam")
            aneg = msb("aneg", [128, 16]); anegB = Buf("aneg")
            small = msb("small", [128, 8]); smallB = Buf("small")
            _q2 = [Buf(f"q_d{i}") for i in range(2)]; q_dB = [_q2[i % 2] for i in range(8)]
            _k2 = [[Buf(f"ksend{j}_{i}") for i in range(2)] for j in range(4)]
            ksB = [[_k2[j][i % 2] for i in range(8)] for j in range(4)]; kaB = [Buf(f"kall{j}") for j in range(4)]
            vsB = [Buf(f"vsend{t}") for t in range(8)]; vaB = [Buf(f"vall{j}") for j in range(4)]
            _z2 = [Buf(f"zs{i}") for i in range(2)]; zsB = [_z2[i % 2] for i in range(8)]
            _x2 = [Buf(f"xfm{i}") for i in range(2)]; xfmB = [_x2[i % 2] for i in range(8)]
            _t2 = [Buf(f"xtok{i}") for i in range(2)]; xtokB = [_t2[i % 2] for i in range(8)]
            cfmB = [Buf(f"cfm{i}") for i in range(2)]; bfmB = [Buf(f"bfm{i}") for i in range(2)]; btokB = [Buf(f"btok{i}") for i in range(2)]
            fsB = [Buf("fsend0"), Buf("fsend1"), Buf("dsend")]
            faB = [Buf("fall0"), Buf("fall1"), Buf("dall")]

            lo, _ = PK["lam"]
            S.op("dve", lambda e: e.tensor_tensor(out=small[:, 0:1], in0=pk[:, lo:lo + 1], in1=pk[:, lo + 1:lo + 2], op=ALU.mult),
                 reads=[cB], writes=[smallB])
            S.op("dve", lambda e: e.tensor_tensor(out=small[:, 1:2], in0=pk[:, lo + 2:lo + 3], in1=pk[:, lo + 3:lo + 4], op=ALU.mult),
                 reads=[cB], writes=[smallB])
            S.op("pe", lambda e: e.matmul(PS[7][:, 0:2], ones_f, small[:, 0:2], start=True, stop=True),
                 reads=[smallB, cB], writes=[PB[7]])
            S.op("act", lambda e: e.activation(out=small[:, 2:4], in_=PS[7][:, 0:2], func=AF.Exp), reads=[PB[7]], writes=[smallB])
            S.op("dve", lambda e: e.tensor_tensor(out=small[:, 4:5], in0=small[:, 3:4], in1=small[:, 2:3], op=ALU.subtract),
                 reads=[smallB], writes=[smallB])
            S.op("dve", lambda e: e.tensor_scalar(out=nlam[:, 0:1], in0=small[:, 4:5], scalar1=-0.2, scalar2=None, op0=ALU.add),
                 reads=[smallB], writes=[nlamB])
            ao, _ = PK["alog"]
            S.op("act", lambda e: e.activation(out=aneg[:], in_=pk[:, ao:ao + 16], func=AF.Exp), reads=[cB], writes=[anegB])
            S.op("dve", lambda e: e.tensor_scalar(out=aneg[:], in0=aneg[:], scalar1=-1.0, scalar2=None, op0=ALU.mult),
                 reads=[anegB], writes=[anegB])

            with contextlib.ExitStack() as ist:
                def isb(name, shape, dt=F32):
                    return ist.enter_context(nc.sbuf_tensor("i_" + name, list(shape), dt))
                hn = isb("hn", [128, DC, NT], BF16); hnB = Buf("ihn")
                wvb = isb("wvb", [128, DC, 1024], BF16); wvB = Buf("wvb")
                wdtb = isb("wdtb", [128, DC, 16], BF16); wdtB = Buf("wdtb")
                wt = [isb(f"wt{i}", [128, DC, 128], BF16) for i in range(2)]
                wtB = [Buf(f"wt{i}") for i in range(2)]
                cosT = isb("cosT", [128, NT]); sinT = isb("sinT", [128, NT]); csB = Buf("cs")
                sq = [isb(f"sq{i}", [128, 512], BF16) for i in range(2)]
                sqB = [Buf(f"isq{i}") for i in range(2)]
                lt2 = [isb(f"lt2{i}", [128, 512]) for i in range(2)]; lt2B = [Buf(f"lt2{i}") for i in range(2)]
                rs2 = [isb(f"rs2{i}", [128, 512]) for i in range(2)]; rs2B = [Buf(f"rs2{i}") for i in range(2)]
                rt, rtB, rs, rsB = lt2[0], lt2B[0], rs2[0], rs2B[0]
                qn32 = [isb(f"qn32{i}", [128, 512]) for i in range(2)]; qn32B = [Buf(f"qn32{i}") for i in range(2)]
                qnb = [isb(f"qnb{i}", [128, 512], BF16) for i in range(2)]; qnbB = [Buf(f"qnb{i}") for i in range(2)]
                t1 = [isb(f"t1{i}", [128, 512]) for i in range(2)]; t1B = [Buf(f"t1{i}") for i in range(2)]
                t2 = [isb(f"t2{i}", [128, 512]) for i in range(2)]; t2B = [Buf(f"t2{i}") for i in range(2)]
                qr = [isb(f"qr{i}", [128, 512], BF16) for i in range(2)]
                qrB = [Buf(f"qr{i}") for i in range(2)]
                vt = [isb(f"vt{i}", [128, 1024], BF16) for i in range(2)]
                vtB = [Buf(f"vt{i}") for i in range(2)]
                dtt = isb("dtt", [128, 16]); dttB = Buf("dtt")
                convin = isb("convin", [128, 3 + NR]); cinB = Buf("convin")
                convm = isb("convm", [128, 3 + 16]); cmB = Buf("convm")
                acc0 = isb("acc0", [128, NR]); acc = [acc0, acc0]
                accB0 = Buf("acc0"); accB = [accB0, accB0]
                accm0 = isb("accm0", [128, 16]); accm = [accm0, accm0]
                accmB0 = Buf("accm0"); accmB = [accmB0, accmB0]
                xc = isb("xc", [128, NR + 16]); xcB = Buf("xc")
                xcb = isb("xcb", [128, NR + 16], BF16); xcbB = Buf("xcb")
                xtk = isb("xtk", [128, 9, 128]); xtkB = Buf("xtk")
                btk = isb("btk", [128, 9, 128], BF16); btkB = Buf("btk")

                S.dma("pool", wvb[:], winv[:, :, :], writes=[wvB])
                S.dma("pool", wdtb[:], windt[:, :, :], writes=[wdtB])
                S.dma("sp", cosT[:], cos_d[:, :], writes=[csB])
                S.dma("sp", sinT[:], sin_d[:, :], writes=[csB])
                S.op("dve", lambda e: e.memset(convm[:], 0.0), writes=[cmB])
                S.op("dve", lambda e: e.memset(kmeta[:], 0.0), writes=[kmB])

                for (t0, tn) in TCH:
                    for c in range(DC):
                        S.op("act", lambda e, c=c: e.activation(out=sq[c % 2][:, 0:tn], in_=h[:, c, t0:t0 + tn], func=AF.Square),
                             reads=[hB], writes=[sqB[c % 2]])
                        S.op("pe", lambda e, c=c: e.matmul(PS[7][:, 0:tn], ones_b, sq[c % 2][:, 0:tn],
                                                           start=(c == 0), stop=(c == DC - 1)),
                             reads=[sqB[c % 2], cB], writes=[PB[7]])
                    S.op("act", lambda e: e.activation(out=rt[:, 0:tn], in_=PS[7][:, 0:tn], func=AF.Sqrt, scale=1.0 / D, bias=EPS),
                         reads=[PB[7]], writes=[rtB])
                    S.op("dve", lambda e: e.reciprocal(out=rs[:, 0:tn], in_=rt[:, 0:tn]), reads=[rtB], writes=[rsB])
                    for c in range(DC):
                        S.op("dve", lambda e, c=c: e.scalar_tensor_tensor(
                            out=hn[:, c, t0:t0 + tn], in0=h[:, c, t0:t0 + tn], scalar=pkc("gm", c),
                            in1=rs[:, 0:tn], op0=ALU.mult, op1=ALU.mult), reads=[hB, rsB, cB], writes=[hnB])

                dbo, _ = PK["dtb"]
                ib = 0
                for tb in range(9):
                    t0 = tb * 128
                    m = 128 if tb < 8 else 16
                    vsl = tb % 2
                    for half in range(2):
                        ba = ib % 4; ib += 1
                        for k in range(DC):
                            S.op("pe", lambda e, k=k: e.matmul(PS[ba][0:m, :], hn[:, k, t0:t0 + m], wvb[:, k, half * 512:(half + 1) * 512],
                                                               start=(k == 0), stop=(k == DC - 1)),
                                 reads=[hnB, wvB], writes=[PB[ba]])
                        if tb < 8:
                            S.op("act", lambda e: e.copy(out=vt[vsl][:, half * 512:(half + 1) * 512], in_=PS[ba][:, :]),
                                 reads=[PB[ba]], writes=[vtB[vsl]])
                        else:
                            S.op("act", lambda e: e.copy(out=vmeta[0:16, half * 512:(half + 1) * 512], in_=PS[ba][0:16, :]),
                                 reads=[PB[ba]], writes=[vmB])
                    if tb < 8:
                        S.dma("sp", vsend[tb // 2][(tb % 2) * 128:(tb % 2) * 128 + 128, :], vt[vsl][:], reads=[vtB[vsl]], writes=[vsB[tb]])
                    for k in range(DC):
                        S.op("pe", lambda e, k=k: e.matmul(PS[7][0:m, 0:16], hn[:, k, t0:t0 + m], wdtb[:, k, :],
                                                           start=(k == 0), stop=(k == DC - 1)),
                             reads=[hnB, wdtB], writes=[PB[7]])
                    S.op("dve", lambda e: e.tensor_tensor(out=dtt[0:m, :], in0=PS[7][0:m, 0:16], in1=pk[0:m, dbo:dbo + 16], op=ALU.add),
                         reads=[PB[7], cB], writes=[dttB])
                    S.op("act", lambda e: e.activation(out=dtt[0:m, :], in_=dtt[0:m, :], func=AF.Exp), reads=[dttB], writes=[dttB])
                    S.op("act", lambda e: e.activation(out=dt_sb[0:m, tb, :], in_=dtt[0:m, :], func=AF.Ln, bias=1.0, scale=1.0),
                         reads=[dttB], writes=[dtB])
                    S.op("dve", lambda e: e.tensor_tensor(out=dA_sb[0:m, tb, :], in0=dt_sb[0:m, tb, :], in1=aneg[0:m, :], op=ALU.mult),
                         reads=[dtB, anegB], writes=[dAB])

                ia = 0
                _order = []
                for n_ in range(16):
                    _order.append(n_)
                    if n_ < 12:
                        _order.append(24 + n_)
                    if n_ % 2 == 0:
                        _order.append(16 + n_ // 2)
                assert sorted(_order) == list(range(36))
                for wi_, i in enumerate(_order):
                    kind = "q" if i < 8 else "k" if i < 16 else "z" if i < 24 else "x" if i < 32 else "B" if i < 34 else "C"
                    sl = wi_ % 2
                    S.dma("pool", wt[sl][:], winfm[i], writes=[wtB[sl]])
                    tcs = TCH[:2] if kind in ("q", "z") else TCH
                    for (t0, tn) in tcs:
                        ba = ia % 4; ia += 1
                        for k in range(DC):
                            S.op("pe", lambda e, k=k: e.matmul(PS[ba][:, 0:tn], wt[sl][:, k, :], hn[:, k, t0:t0 + tn],
                                                               start=(k == 0), stop=(k == DC - 1)),
                                 reads=[wtB[sl], hnB], writes=[PB[ba]])
                        if kind in ("q", "k"):
                            hd = i % 8
                            gname = "gq" if kind == "q" else "gk"
                            u2 = ia % 2
                            bss, brot = 4 + u2, 6 + u2
                            S.op("act", lambda e: e.activation(out=sq[u2][:, 0:tn], in_=PS[ba][:, 0:tn], func=AF.Square),
                                 reads=[PB[ba]], writes=[sqB[u2]])
                            S.op("pe", lambda e: e.matmul(PS[bss][:, 0:tn], bd64_b, sq[u2][:, 0:tn], start=True, stop=True),
                                 reads=[sqB[u2], cB], writes=[PB[bss]])
                            S.op("act", lambda e: e.activation(out=lt2[u2][:, 0:tn], in_=PS[bss][:, 0:tn], func=AF.Ln, scale=1.0 / 64, bias=EPS),
                                 reads=[PB[bss]], writes=[lt2B[u2]])
                            S.op("act", lambda e: e.activation(out=rs2[u2][:, 0:tn], in_=lt2[u2][:, 0:tn], func=AF.Exp, scale=-0.5),
                                 reads=[lt2B[u2]], writes=[rs2B[u2]])
                            S.op("dve", lambda e: e.scalar_tensor_tensor(out=qn32[u2][:, 0:tn], in0=PS[ba][:, 0:tn], scalar=pkc(gname),
                                                                         in1=rs2[u2][:, 0:tn], op0=ALU.mult, op1=ALU.mult),
                                 reads=[PB[ba], rs2B[u2], cB], writes=[qn32B[u2]])
                            S.op("act", lambda e: e.copy(out=qnb[u2][:, 0:tn], in_=qn32[u2][:, 0:tn]), reads=[qn32B[u2]], writes=[qnbB[u2]])
                            S.op("pe", lambda e: e.matmul(PS[brot][:, 0:tn], rot_b, qnb[u2][:, 0:tn], start=True, stop=True),
                                 reads=[qnbB[u2], cB], writes=[PB[brot]])
                            S.op("dve", lambda e: e.tensor_tensor(out=t1[u2][:, 0:tn], in0=qn32[u2][:, 0:tn], in1=cosT[:, t0:t0 + tn], op=ALU.mult),
                                 reads=[qn32B[u2], csB], writes=[t1B[u2]])
                            S.op("dve", lambda e: e.tensor_tensor(out=t2[u2][:, 0:tn], in0=PS[brot][:, 0:tn], in1=sinT[:, t0:t0 + tn], op=ALU.mult),
                                 reads=[PB[brot], csB], writes=[t2B[u2]])
                            qs = ia % 2
                            S.op("dve", lambda e: e.tensor_tensor(out=qr[qs][:, 0:tn], in0=t1[u2][:, 0:tn], in1=t2[u2][:, 0:tn], op=ALU.add),
                                 reads=[t1B[u2], t2B[u2]], writes=[qrB[qs]])
                            if kind == "q":
                                S.dma("sp", q_d[hd, :, t0:t0 + tn], qr[qs][:, 0:tn], reads=[qrB[qs]], writes=[q_dB[hd]])
                            elif t0 < NR:
                                for jj in range(2):
                                    jq = t0 // 256 + jj
                                    S.dma("sp", ksend[jq][hd * 128:(hd + 1) * 128, :], qr[qs][:, jj * 256:(jj + 1) * 256],
                                          reads=[qrB[qs]], writes=[ksB[jq][hd]])
                            else:
                                S.op("act", lambda e: e.copy(out=kmeta[0:64, hd, 0, :], in_=qr[qs][0:64, 0:16]), reads=[qrB[qs]], writes=[kmB])
                                S.op("act", lambda e: e.copy(out=kmeta[64:128, hd, 1, :], in_=qr[qs][64:128, 0:16]), reads=[qrB[qs]], writes=[kmB])
                        elif kind == "z":
                            qs = ia % 2
                            S.op("act", lambda e: e.activation(out=qr[qs][:, 0:tn], in_=PS[ba][:, 0:tn], func=AF.Silu),
                                 reads=[PB[ba]], writes=[qrB[qs]])
                            S.dma("sp", zs_d[i - 16, :, t0:t0 + tn], qr[qs][:, 0:tn], reads=[qrB[qs]], writes=[zsB[i - 16]])
                        else:
                            if t0 < NR:
                                S.op("act", lambda e: e.copy(out=convin[:, 3 + t0:3 + t0 + tn], in_=PS[ba][:, 0:tn]),
                                     reads=[PB[ba]], writes=[cinB])
                            else:
                                S.op("act", lambda e: e.copy(out=convin[:, 0:3], in_=PS[ba][:, 16:19]), reads=[PB[ba]], writes=[cinB])
                                S.op("act", lambda e: e.copy(out=convm[:, 3:19], in_=PS[ba][:, 0:16]), reads=[PB[ba]], writes=[cmB])
                    if kind in ("x", "B", "C"):
                        ci = i - 24
                        cwo, _ = PK["cw"]
                        cbo, _ = PK["cb"]
                        for (src, srcB, ac, acB, n, o0) in ((convin, cinB, acc, accB, NR, 0), (convm, cmB, accm, accmB, 16, NR)):
                            if kind == "C" and n == 16:
                                continue
                            S.op("dve", lambda e: e.tensor_scalar(out=ac[0][:, 0:n], in0=src[:, 0:n], scalar1=pk[:, cwo + ci * 4:cwo + ci * 4 + 1],
                                                                  scalar2=None, op0=ALU.mult), reads=[srcB, cB], writes=[acB[0]])
                            for j in range(1, 4):
                                S.op("dve", lambda e, j=j: e.scalar_tensor_tensor(
                                    out=ac[j % 2][:, 0:n], in0=src[:, j:j + n], scalar=pk[:, cwo + ci * 4 + j:cwo + ci * 4 + j + 1],
                                    in1=ac[(j - 1) % 2][:, 0:n], op0=ALU.mult, op1=ALU.add),
                                    reads=[srcB, cB, acB[(j - 1) % 2]], writes=[acB[j % 2]])
                            S.op("act", lambda e: e.activation(out=xc[:, o0:o0 + n], in_=ac[1][:, 0:n], func=AF.Silu,
                                                               bias=pk[:, cbo + ci:cbo + ci + 1], scale=1.0),
                                 reads=[acB[1], cB], writes=[xcB])
                        nv = NR + 16 if kind != "C" else NR
                        if kind == "x":
                            j = i - 24
                            S.dma("sp", xfm_d[j, :, :], xc[:, 0:NR], reads=[xcB], writes=[xfmB[j]])
                            for gi, blks in enumerate(((0, 1, 2, 3), (4, 5, 6, 7), (8,))):
                                bt = 4 + gi
                                w = 128 if gi < 2 else 16
                                for kk_, blk in enumerate(blks):
                                    S.op("pe", lambda e: e.transpose(PS[bt][0:w, kk_ * 128:(kk_ + 1) * 128], xc[:, blk * 128:blk * 128 + w], ident_f),
                                         reads=[xcB, cB], writes=[PB[bt]])
                                nb = len(blks)
                                S.op("act", lambda e: e.copy(out=xtk[0:w, blks[0]:blks[0] + nb, :],
                                                             in_=PS[bt][0:w, 0:nb * 128].rearrange("p (b f) -> p b f", b=nb)),
                                     reads=[PB[bt]], writes=[xtkB])
                            S.dma("sp", xtok_d[0:8, :, j * 128:(j + 1) * 128].rearrange("b p f -> p b f"), xtk[:, 0:8, :],
                                  reads=[xtkB], writes=[xtokB[j]])
                            S.dma("sp", xtok_d[8, 0:16, j * 128:(j + 1) * 128], xtk[0:16, 8, :], reads=[xtkB], writes=[xtokB[j]])
                        else:
                            g = (i - 32) % 2
                            S.op("act", lambda e: e.copy(out=xcb[:, 0:nv], in_=xc[:, 0:nv]), reads=[xcB], writes=[xcbB])
                            if kind == "B":
                                S.dma("sp", bfm_d[g, :, :], xcb[:, :], reads=[xcbB], writes=[bfmB[g]])
                                for gi, blks in enumerate(((0, 1, 2, 3), (4, 5, 6, 7), (8,))):
                                    bt = 4 + gi
                                    w = 128 if gi < 2 else 16
                                    for kk_, blk in enumerate(blks):
                                        S.op("pe", lambda e: e.transpose(PS[bt][0:w, kk_ * 128:(kk_ + 1) * 128], xc[:, blk * 128:blk * 128 + w], ident_f),
                                             reads=[xcB, cB], writes=[PB[bt]])
                                    nb = len(blks)
                                    S.op("act", lambda e: e.copy(out=btk[0:w, blks[0]:blks[0] + nb, :],
                                                                 in_=PS[bt][0:w, 0:nb * 128].rearrange("p (b f) -> p b f", b=nb)),
                                         reads=[PB[bt]], writes=[btkB])
                                S.dma("sp", btok_d[0:8, :, g * 128:(g + 1) * 128].rearrange("b p f -> p b f"), btk[:, 0:8, :],
                                      reads=[btkB], writes=[btokB[g]])
                                S.dma("sp", btok_d[8, 0:16, g * 128:(g + 1) * 128], btk[0:16, 8, :], reads=[btkB], writes=[btokB[g]])
                            else:
                                S.dma("sp", cfm_d[g, :, 0:NR], xcb[:, 0:NR], reads=[xcbB], writes=[cfmB[g]])

                if USE_CC:
                    for j in range(4):
                        S.dmaish("pool", lambda e, j=j: e.collective_compute("AllGather", ALU.bypass, replica_groups=GROUPS,
                                                                              ins=[ksend[j][:, :]], outs=[kall[j][:, :]]),
                                 reads=ksB[j], writes=[kaB[j]], inc=1)
                        S.dmaish("pool", lambda e, j=j: e.collective_compute("AllGather", ALU.bypass, replica_groups=GROUPS,
                                                                              ins=[vsend[j][:, :]], outs=[vall[j][:, :]]),
                                 reads=[vsB[2 * j], vsB[2 * j + 1]], writes=[vaB[j]], inc=1)
                if stop_after == "inproj":
                    dump("q", q_d, [8, 128, NR], BF16, q_dB)
                    if USE_CC:
                        dump("kall0", kall[0], [4 * NR, 256], BF16, [kaB[0]])
                        dump("vall3", vall[3], [4 * 256, NR], BF16, [vaB[3]])
                    dump("zs", zs_d, [8, 128, NR], BF16, zsB)
                    dump("xfm", xfm_d, [8, 128, NR], F32, xfmB)
                    dump("xtok", xtok_d, [9, 128, 1024], F32, xtokB)
                    dump("btok", btok_d, [9, 128, 256], BF16, btokB)
                    dump("bfm", bfm_d, [2, 128, NR + 16], BF16, bfmB)
                    dump("cfm", cfm_d, [2, 128, NR + 16], BF16, cfmB)
                    dump("dt", dt_sb[:], [128, 9, 16], F32, [dtB])
                    dump("kmeta", kmeta[:], [128, 8, 2, 16], BF16, [kmB])
                    dump("vmeta", vmeta[:], [16, 1024], BF16, [vmB])
                    finish()
                    return nc
                S.barrier()
            mix = msb("mix", [128, DC, NR], BF16)
            mixB = Buf("mix")
            with contextlib.ExitStack() as sst:
                def ssb(name, shape, dt=F32):
                    return sst.enter_context(nc.sbuf_tensor("d_" + name, list(shape), dt))
                Sloc = ssb("Sloc", [128, 9, 1024], BF16); SlocB = [Buf(f"Sloc{c}") for c in range(9)]
                Xdt = ssb("Xdt", [128, 8, 1024], BF16); XdtB = [Buf(f"Xdt{c}") for c in range(8)]
                tot_sb = ssb("tot", [128, 9, 16]); totB = Buf("tot")
                dec = ssb("dec", [128, 9, 16]); decB = Buf("dec")
                acs_t = ssb("acs", [128, 9, 16]); acsB = Buf("acs")
                yg = ssb("yg", [128, 8, 128]); ygB = Buf("yg")
                Bf = ssb("Bf", [128, 2, NR + 16], BF16); BfB = Buf("Bf")
                Cf = ssb("Cf", [128, 2, NR + 16], BF16); CfB = Buf("Cf")
                Xt = [ssb(f"Xt{i}", [128, 1024]) for i in range(2)]; XtB = [Buf(f"Xt{i}") for i in range(2)]
                Bt = [ssb(f"Bt{i}", [128, 256], BF16) for i in range(2)]; BtB = [Buf(f"Bt{i}") for i in range(2)]
                Xw = ssb("Xw", [128, 1024], BF16); XwB = Buf("Xw")
                sm = ssb("sm", [128, 4, 16]); smB = Buf("sm")
                Fst = ssb("Fst", [128, 1024]); FstB = Buf("Fst")
                Sbf = ssb("Sbf", [128, 1024], BF16); SbfB = Buf("Sbf")
                Dl = ssb("Dl", [128, 4, 16]); DlB = Buf("Dl")
                coef = ssb("coef", [128, 5, 16]); coefB = Buf("coef")
                Rr = ssb("Rr", [128, 16, 128]); RrB = Buf("Rr")
                Lx = ssb("Lx", [128, 1024]); LxB = Buf("Lx")
                Ex = ssb("Ex", [128, 1024]); ExB = Buf("Ex")
                CBm = ssb("CBm", [128, 128]); CBmB = Buf("CBm")
                MT = ssb("MT", [128, 8, 128], BF16); MTB = Buf("MT")
                Cs = ssb("Cs", [128, 8, 128], BF16); CsB = Buf("Cs")
                xfc = ssb("xfc", [128, 8, 128]); xfcB = Buf("xfc")
                zsc = ssb("zsc", [128, 8, 128], BF16); zscB = Buf("zsc")
                y1 = ssb("y1", [128, 128]); y1B = Buf("y1")
                ssq = ssb("ssq", [128, 1024], BF16); ssqB = Buf("ssq")
                srt = ssb("srt", [128, 128]); srtB = Buf("srt")
                srs = ssb("srs", [128, 128]); srsB = Buf("srs")

                S.dma("sp", Bf[:], bfm_d.rearrange("g p t -> p g t"), reads=bfmB, writes=[BfB])
                S.dma("sp", Cf[:, :, 0:NR], cfm_d.rearrange("g p t -> p g t")[:, :, 0:NR], reads=cfmB, writes=[CfB])

                for c in (8, 0, 1, 2, 3, 4, 5, 6, 7):
                    n = 128 if c < 8 else 16
                    sl = c % 2
                    S.dma("sp", Xt[sl][0:n, :], xtok_d[c, 0:n, :], reads=xtokB, writes=[XtB[sl]])
                    S.dma("act", Bt[sl][0:n, :], btok_d[c, 0:n, :], reads=btokB, writes=[BtB[sl]])
                    S.op("pe", lambda e: e.matmul(PS[7][0:n, 0:16], tle_f[0:n, 0:n], dA_sb[0:n, c, :], start=True, stop=True),
                         reads=[cB, dAB], writes=[PB[7]])
                    S.op("pe", lambda e: e.matmul(PS[7][:, 16:32], ones_f[0:n, :], dA_sb[0:n, c, :], start=True, stop=True),
                         reads=[cB, dAB], writes=[PB[7]])
                    S.op("act", lambda e: e.copy(out=acs_t[0:n, c, :], in_=PS[7][0:n, 0:16]), reads=[PB[7]], writes=[acsB])
                    S.op("act", lambda e: e.copy(out=tot_sb[:, c, :], in_=PS[7][:, 16:32]), reads=[PB[7]], writes=[totB])
                    S.op("act", lambda e: e.activation(out=dec[:, c, :], in_=PS[7][:, 16:32], func=AF.Exp), reads=[PB[7]], writes=[decB])
                    S.op("dve", lambda e: e.tensor_tensor(out=sm[0:n, 0, :], in0=PS[7][0:n, 16:32], in1=acs_t[0:n, c, :], op=ALU.subtract),
                         reads=[PB[7], acsB], writes=[smB])
                    S.op("act", lambda e: e.activation(out=sm[0:n, 1, :], in_=sm[0:n, 0, :], func=AF.Exp), reads=[smB], writes=[smB])
                    S.op("dve", lambda e: e.tensor_tensor(out=sm[0:n, 2, :], in0=sm[0:n, 1, :], in1=dt_sb[0:n, c, :], op=ALU.mult),
                         reads=[smB, dtB], writes=[smB])
                    S.op("dve", lambda e: e.tensor_tensor(out=Xw[0:n, :].rearrange("p (h d) -> p h d", h=16),
                                                          in0=Xt[sl][0:n, :].rearrange("p (h d) -> p h d", h=16),
                                                          in1=bc(sm[0:n, 2, :], 64), op=ALU.mult),
                         reads=[XtB[sl], smB], writes=[XwB])
                    if c < 8:
                        S.op("dve", lambda e: e.tensor_tensor(out=Xdt[:, c, :].rearrange("p (h d) -> p h d", h=16),
                                                              in0=Xt[sl][:, :].rearrange("p (h d) -> p h d", h=16),
                                                              in1=bc(dt_sb[:, c, :], 64), op=ALU.mult),
                             reads=[XtB[sl], dtB], writes=[XdtB[c]])
                    for g in range(2):
                        S.op("pe", lambda e: e.matmul(PS[g][:, :], Bt[sl][0:n, g * 128:(g + 1) * 128], Xw[0:n, g * 512:(g + 1) * 512],
                                                      start=True, stop=True), reads=[BtB[sl], XwB], writes=[PB[g]])
                        S.op("act", lambda e: e.copy(out=Sloc[:, c, g * 512:(g + 1) * 512], in_=PS[g][:, :]),
                             reads=[PB[g]], writes=[SlocB[c]])
                S.op("dve", lambda e: e.tensor_copy(out=Fst[:], in_=Sloc[:, 0, :]), reads=[SlocB[0]], writes=[FstB])
                for c in range(1, 8):
                    S.op("dve", lambda e: e.tensor_tensor(out=Fst[:].rearrange("p (h d) -> p h d", h=16),
                                                          in0=Fst[:].rearrange("p (h d) -> p h d", h=16),
                                                          in1=bc(dec[:, c, :], 64), op=ALU.mult), reads=[FstB, decB], writes=[FstB])
                    S.op("dve", lambda e: e.tensor_tensor(out=Fst[:], in0=Fst[:], in1=Sloc[:, c, :], op=ALU.add),
                         reads=[FstB, SlocB[c]], writes=[FstB])
                S.op("dve", lambda e: e.tensor_tensor(out=sm[:, 3, :], in0=tot_sb[:, 0, :], in1=tot_sb[:, 1, :], op=ALU.add),
                     reads=[totB], writes=[smB])
                for c in range(2, 8):
                    S.op("dve", lambda e: e.tensor_tensor(out=sm[:, 3, :], in0=sm[:, 3, :], in1=tot_sb[:, c, :], op=ALU.add),
                         reads=[totB, smB], writes=[smB])
                S.dma("sp", fsend[0][:, :], Fst[:, 0:512], reads=[FstB], writes=[fsB[0]])
                S.dma("sp", fsend[1][:, :], Fst[:, 512:1024], reads=[FstB], writes=[fsB[1]])
                S.dma("sp", dsend[:, :], sm[:, 3, :], reads=[smB], writes=[fsB[2]])
                if USE_CC:
                    for j, (snd_, rcv_) in enumerate(((fsend[0], fall[0]), (fsend[1], fall[1]), (dsend, dall))):
                        S.dmaish("pool", lambda e, snd_=snd_, rcv_=rcv_: e.collective_compute(
                            "AllGather", ALU.bypass, replica_groups=GROUPS, ins=[snd_[:, :]], outs=[rcv_[:, :]]),
                            reads=[fsB[j]], writes=[faB[j]], inc=1)
                S.dma("sp", Dl[:], dall.rearrange("(r p) f -> p r f", p=128), reads=[faB[2]], writes=[DlB])
                bmo, _ = PK["bm"]
                mko, _ = PK["msk"]
                for q1 in range(5):
                    S.op("dve", lambda e: e.tensor_scalar(out=coef[:, q1, :], in0=Dl[:, 0, :], scalar1=pk[:, bmo + q1:bmo + q1 + 1],
                                                          scalar2=None, op0=ALU.mult), reads=[DlB, cB], writes=[coefB])
                    for q2 in range(1, 4):
                        S.op("dve", lambda e, q2=q2: e.scalar_tensor_tensor(
                            out=coef[:, q1, :], in0=Dl[:, q2, :], scalar=pk[:, bmo + q2 * 5 + q1:bmo + q2 * 5 + q1 + 1],
                            in1=coef[:, q1, :], op0=ALU.mult, op1=ALU.add), reads=[DlB, cB, coefB], writes=[coefB])
                    S.op("act", lambda e: e.activation(out=coef[:, q1, :], in_=coef[:, q1, :], func=AF.Exp), reads=[coefB], writes=[coefB])
                    S.op("dve", lambda e: e.tensor_scalar(out=coef[:, q1, :], in0=coef[:, q1, :], scalar1=pk[:, mko + q1:mko + q1 + 1],
                                                          scalar2=None, op0=ALU.mult), reads=[coefB, cB], writes=[coefB])
                S.op("dve", lambda e: e.tensor_tensor(out=Fst[:].rearrange("p (h d) -> p h d", h=16),
                                                      in0=Sloc[:, 8, :].rearrange("p (h d) -> p h d", h=16),
                                                      in1=bc(coef[:, 4, :], 64), op=ALU.mult), reads=[SlocB[8], coefB], writes=[FstB])
                for q1 in range(4):
                    S.dma("sp", Ex[:, 0:512], fall[0][q1 * 128:(q1 + 1) * 128, :], reads=[faB[0]], writes=[ExB])
                    S.dma("act", Ex[:, 512:1024], fall[1][q1 * 128:(q1 + 1) * 128, :], reads=[faB[1]], writes=[ExB])
                    S.op("dve", lambda e: e.tensor_tensor(out=Lx[:].rearrange("p (h d) -> p h d", h=16),
                                                          in0=Ex[:].rearrange("p (h d) -> p h d", h=16),
                                                          in1=bc(coef[:, q1, :], 64), op=ALU.mult), reads=[ExB, coefB], writes=[LxB])
                    S.op("dve", lambda e: e.tensor_tensor(out=Fst[:], in0=Fst[:], in1=Lx[:], op=ALU.add), reads=[FstB, LxB], writes=[FstB])

                dso, _ = PK["dsk"]
                for c in range(8):
                    cs = slice(c * 128, (c + 1) * 128)
                    S.op("act", lambda e: e.copy(out=Sbf[:], in_=Fst[:]), reads=[FstB], writes=[SbfB])
                    S.op("dve", lambda e: e.tensor_tensor(out=Rr[:], in0=bc(dA_sb[:, c, :], 128), in1=bcm(tle_f, 16), op=ALU.mult),
                         reads=[dAB, cB], writes=[RrB])
                    S.dma("sp", xfc[:], xfm_d[:, :, cs].rearrange("i p t -> p i t"), reads=xfmB, writes=[xfcB])
                    S.dma("act", zsc[:], zs_d[:, :, cs].rearrange("i p t -> p i t"), reads=zsB, writes=[zscB])
                    for g in range(2):
                        rr = Rr[:, 8 * g:8 * g + 8, :].rearrange("p h l -> p (h l)")
                        for hf in range(2):
                            S.op("pe", lambda e: e.matmul(PS[hf][:, :], tgt_f, rr[:, hf * 512:(hf + 1) * 512], start=True, stop=True),
                                 reads=[cB, RrB], writes=[PB[hf]])
                            S.op("pe", lambda e: e.matmul(PS[2 + hf][:, :], ones_f, rr[:, hf * 512:(hf + 1) * 512], start=True, stop=True),
                                 reads=[cB, RrB], writes=[PB[2 + hf]])
                            S.op("act", lambda e: e.activation(out=Lx[:, hf * 512:(hf + 1) * 512], in_=PS[hf][:, :], func=AF.Exp),
                                 reads=[PB[hf]], writes=[LxB])
                            S.op("act", lambda e: e.activation(out=Ex[:, hf * 512:(hf + 1) * 512], in_=PS[2 + hf][:, :], func=AF.Exp),
                                 reads=[PB[2 + hf]], writes=[ExB])
                        S.op("pe", lambda e: e.matmul(PS[4][:, 0:128], Bf[:, g, cs], Cf[:, g, cs], start=True, stop=True),
                             reads=[BfB, CfB], writes=[PB[4]])
                        S.op("dve", lambda e: e.tensor_tensor(out=CBm[:], in0=PS[4][:, 0:128], in1=tle_f, op=ALU.mult),
                             reads=[PB[4], cB], writes=[CBmB])
                        S.op("dve", lambda e: e.tensor_tensor(out=MT[:], in0=Lx[:].rearrange("p (h l) -> p h l", h=8),
                                                              in1=bcm(CBm[:], 8), op=ALU.mult), reads=[LxB, CBmB], writes=[MTB])
                        S.op("dve", lambda e: e.tensor_tensor(out=Cs[:], in0=Ex[:].rearrange("p (h l) -> p h l", h=8),
                                                              in1=bcm(Cf[:, g, cs], 8), op=ALU.mult), reads=[ExB, CfB], writes=[CsB])
                        yb = 5 + g
                        for hh in range(8):
                            hd = 8 * g + hh
                            half = hd % 2
                            ii = hh // 2
                            oap = PS[yb][half * 64:(half + 1) * 64, ii * 128:(ii + 1) * 128]
                            S.op("pe", lambda e: e.matmul(oap, Xdt[:, c, hd * 64:(hd + 1) * 64], MT[:, hh, :], start=True, stop=False),
                                 reads=[XdtB[c], MTB], writes=[PB[yb]])
                            S.op("pe", lambda e: e.matmul(oap, Sbf[:, hd * 64:(hd + 1) * 64], Cs[:, hh, :], start=False, stop=True),
                                 reads=[SbfB, CsB], writes=[PB[yb]])
                        for ii in range(4):
                            i = 4 * g + ii
                            S.op("dve", lambda e: e.scalar_tensor_tensor(out=y1[:], in0=xfc[:, i, :], scalar=pk[:, dso + i:dso + i + 1],
                                                                         in1=PS[yb][:, ii * 128:(ii + 1) * 128], op0=ALU.mult, op1=ALU.add),
                                 reads=[xfcB, cB, PB[yb]], writes=[y1B])
                            S.op("dve", lambda e: e.tensor_tensor(out=yg[:, i, :], in0=y1[:], in1=zsc[:, i, :], op=ALU.mult),
                                 reads=[y1B, zscB], writes=[ygB])
                    S.op("act", lambda e: e.activation(out=ssq[:], in_=yg[:].rearrange("p i t -> p (i t)"), func=AF.Square),
                         reads=[ygB], writes=[ssqB])
                    for i in range(8):
                        S.op("pe", lambda e, i=i: e.matmul(PS[7][:, 0:128], ones_b, ssq[:, i * 128:(i + 1) * 128], start=(i == 0), stop=(i == 7)),
                             reads=[ssqB, cB], writes=[PB[7]])
                    S.op("act", lambda e: e.activation(out=srt[:], in_=PS[7][:, 0:128], func=AF.Sqrt, scale=1.0 / 1024, bias=EPS),
                         reads=[PB[7]], writes=[srtB])
                    S.op("dve", lambda e: e.reciprocal(out=srs[:], in_=srt[:]), reads=[srtB], writes=[srsB])
                    for i in range(8):
                        S.op("dve", lambda e, i=i: e.scalar_tensor_tensor(out=mix[:, 8 + i, cs], in0=yg[:, i, :],
                                                                          scalar=pkc("gssd", i), in1=srs[:], op0=ALU.mult, op1=ALU.mult),
                             reads=[ygB, srsB, cB], writes=[mixB])
                    S.op("dve", lambda e: e.tensor_tensor(out=Fst[:].rearrange("p (h d) -> p h d", h=16),
                                                          in0=Fst[:].rearrange("p (h d) -> p h d", h=16),
                                                          in1=bc(dec[:, c, :], 64), op=ALU.mult), reads=[FstB, decB, SbfB], writes=[FstB])
                    S.op("dve", lambda e: e.tensor_tensor(out=Fst[:], in0=Fst[:], in1=Sloc[:, c, :], op=ALU.add),
                         reads=[FstB, SlocB[c]], writes=[FstB])
                if stop_after == "ssd":
                    dump("acs", acs_t[:, 0:8, :], [128, 8, 16], F32, [acsB])
                    dump("dec", dec[:], [128, 9, 16], F32, [decB])
                    dump("tot", tot_sb[:], [128, 9, 16], F32, [totB])
                    dump("Sloc", Sloc[:], [128, 9, 1024], BF16, SlocB)
                    dump("Send", Fst[:], [128, 1024], F32, [FstB])
                    dump("Lx", Lx[:], [128, 1024], F32, [LxB])
                    dump("Ex", Ex[:], [128, 1024], F32, [ExB])
                    dump("CBm", CBm[:], [128, 128], F32, [CBmB])
                    dump("MT", MT[:], [128, 8, 128], BF16, [MTB])
                    dump("Cs", Cs[:], [128, 8, 128], BF16, [CsB])
                    dump("yg", yg[:], [128, 8, 128], F32, [ygB])
                    dump("Xdt", Xdt[:], [128, 8, 1024], BF16, XdtB)
                    dump("Sbf", Sbf[:], [128, 1024], BF16, [SbfB])
                    dump("coef", coef[:], [128, 5, 16], F32, [coefB])
                S.barrier()
            if stop_after == "ssd":
                dump("mix", mix[:, 8:16, :], [128, 8, NR], BF16, [mixB])
                dump("dt", dt_sb[:, 0:8, :], [128, 8, 16], F32, [dtB])
                dump("dA", dA_sb[:, 0:8, :], [128, 8, 16], F32, [dAB])
                dump("xtok", xtok_d[0:8], [8, 128, 1024], F32, xtokB)
                dump("btok", btok_d[0:8], [8, 128, 256], BF16, btokB)
                dump("bfm", bfm_d, [2, 128, NR + 16], BF16, bfmB)
                dump("cfm", cfm_d[:, :, 0:NR], [2, 128, NR], BF16, cfmB)
                dump("xfm", xfm_d, [8, 128, NR], F32, xfmB)
                dump("zs", zs_d, [8, 128, NR], BF16, zsB)
                finish()
                return nc
            with contextlib.ExitStack() as ast_:
                def asb(name, shape, dt=F32):
                    return ast_.enter_context(nc.sbuf_tensor("a_" + name, list(shape), dt))
                Kb = [[asb(f"K{sl}{s}", [128, 4 * NR], BF16) for s in range(2)] for sl in range(2)]
                KB = [[Buf(f"K{sl}{s}") for s in range(2)] for sl in range(2)]
                Qb = [[asb(f"Q{sl}{s}", [128, NR], BF16) for s in range(2)] for sl in range(2)]
                QB = [[Buf(f"Q{sl}{s}") for s in range(2)] for sl in range(2)]
                Vb = [asb(f"V{sl}", [128, 32, 128], BF16) for sl in range(2)]
                VB = [Buf(f"V{sl}") for sl in range(2)]
                NE = 6
                Eb = [asb(f"E{i}", [128, 512], BF16) for i in range(NE)]
                EB = [Buf(f"E{i}") for i in range(NE)]
                r0 = asb("r0", [128, 512]); r0B = Buf("r0")
                r1 = asb("r1", [128, 512]); r1B = Buf("r1")
                o0 = asb("o0", [128, 512]); o0B = Buf("o0")
                o1 = asb("o1", [128, 512]); o1B = Buf("o1")
                osq = asb("osq", [128, 512], BF16); osqB = Buf("osq")
                art = asb("art", [128, 512]); artB = Buf("art")
                ars = asb("ars", [128, 512]); arsB = Buf("ars")
                kview = [kall[j].rearrange("(r n) t -> n r t", r=4) for j in range(4)]
                for sl in range(2):
                    S.dma("sp", Kb[sl][0][64:128, :], umask_d[:, :], writes=[KB[sl][0]])
                    S.dma("sp", Kb[sl][1][0:64, :], umask_d[:, :], writes=[KB[sl][1]])
                    S.dma("sp", Qb[sl][0][64:128, :], wmask_d[:, :], writes=[QB[sl][0]])
                    S.dma("sp", Qb[sl][1][0:64, :], wmask_d[:, :], writes=[QB[sl][1]])
                for hd in range(8):
                    sl = hd % 2
                    for j in range(4):
                        S.dma("sp", Kb[sl][0][0:64, j * 1024:(j + 1) * 1024].rearrange("p (r t) -> p r t", r=4),
                              kview[j][hd * 128:hd * 128 + 64], reads=[kaB[j]], writes=[KB[sl][0]])
                        S.dma("sp", Kb[sl][1][64:128, j * 1024:(j + 1) * 1024].rearrange("p (r t) -> p r t", r=4),
                              kview[j][hd * 128 + 64:hd * 128 + 128], reads=[kaB[j]], writes=[KB[sl][1]])
                        S.dma("act", Vb[sl][:, j * 8:(j + 1) * 8, :],
                              vall[j].rearrange("(b p) f -> p b f", p=128)[:, :, hd * 128:(hd + 1) * 128],
                              reads=[vaB[j]], writes=[VB[sl]])
                    S.dma("act", Qb[sl][0][0:64, :], q_d[hd, 0:64, :], reads=[q_dB[hd]], writes=[QB[sl][0]])
                    S.dma("act", Qb[sl][1][64:128, :], q_d[hd, 64:128, :], reads=[q_dB[hd]], writes=[QB[sl][1]])
                    for qb in range(2):
                        qs = slice(qb * 512, (qb + 1) * 512)
                        units = [(kb, s) for kb in range(33) for s in range(2)]

                        def opnds(kb, s):
                            if kb < 32:
                                return (Kb[sl][s][:, kb * 128:(kb + 1) * 128], [KB[sl][s]], Vb[sl][:, kb, :], [VB[sl]], 128)
                            return (kmeta[:, hd, s, :], [kmB], vmeta[0:16, hd * 128:(hd + 1) * 128], [vmB], 16)

                        def emit_S(i):
                            kb, s = units[i]
                            lk, rdk, lv, rdv, nk = opnds(kb, s)
                            bs = 4 + i % 4
                            S.op("pe", lambda e: e.matmul(PS[bs][0:nk, :], lk, Qb[sl][s][:, qs], start=True, stop=True),
                                 reads=rdk + [QB[sl][s]], writes=[PB[bs]])

                        def emit_E(i):
                            kb, s = units[i]
                            nk = 128 if kb < 32 else 16
                            bs = 4 + i % 4
                            S.op("act", lambda e: e.activation(out=Eb[i % NE][0:nk, :], in_=PS[bs][0:nk, :], func=AF.Exp, scale=0.125),
                                 reads=[PB[bs]], writes=[EB[i % NE]])

                        def emit_PV(i):
                            kb, s = units[i]
                            lk, rdk, lv, rdv, nk = opnds(kb, s)
                            S.op("pe", lambda e: e.matmul(PS[s][:, :], lv, Eb[i % NE][0:nk, :], start=(kb == 0), stop=(kb == 32)),
                                 reads=rdv + [EB[i % NE]], writes=[PB[s]])
                            S.op("pe", lambda e: e.matmul(PS[2 + s][:, :], ones_b[0:nk, :], Eb[i % NE][0:nk, :], start=(kb == 0), stop=(kb == 32)),
                                 reads=[cB, EB[i % NE]], writes=[PB[2 + s]])

                        emit_S(0)
                        emit_S(1)
                        emit_S(2)
                        for i in range(len(units)):
                            emit_E(i)
                            if i + 3 < len(units):
                                emit_S(i + 3)
                            emit_PV(i)
                        S.op("dve", lambda e: e.tensor_copy(out=o0[:], in_=PS[0][:, :]), reads=[PB[0]], writes=[o0B])
                        S.op("dve", lambda e: e.tensor_copy(out=o1[:], in_=PS[1][:, :]), reads=[PB[1]], writes=[o1B])
                        S.op("dve", lambda e: e.tensor_copy(out=r0[:], in_=PS[2][:, :]), reads=[PB[2]], writes=[r0B])
                        S.op("dve", lambda e: e.tensor_copy(out=r1[:], in_=PS[3][:, :]), reads=[PB[3]], writes=[r1B])
                        S.op("dve", lambda e: e.reciprocal(out=r0[:], in_=r0[:]), reads=[r0B], writes=[r0B])
                        S.op("dve", lambda e: e.reciprocal(out=r1[:], in_=r1[:]), reads=[r1B], writes=[r1B])
                        S.op("dve", lambda e: e.tensor_tensor(out=o0[:], in0=o0[:], in1=r0[:], op=ALU.mult),
                             reads=[o0B, r0B], writes=[o0B])
                        S.op("dve", lambda e: e.tensor_tensor(out=o1[:], in0=o1[:], in1=r1[:], op=ALU.mult),
                             reads=[o1B, r1B], writes=[o1B])
                        S.op("dve", lambda e: e.scalar_tensor_tensor(out=o0[:], in0=o1[:], scalar=nlam[:, 0:1], in1=o0[:],
                                                                     op0=ALU.mult, op1=ALU.add),
                             reads=[o1B, o0B, nlamB], writes=[o0B])
                        S.op("act", lambda e: e.activation(out=osq[:], in_=o0[:], func=AF.Square), reads=[o0B], writes=[osqB])
                        S.op("pe", lambda e: e.matmul(PS[7][:, :], ones_b, osq[:], start=True, stop=True),
                             reads=[osqB, cB], writes=[PB[7]])
                        S.op("act", lambda e: e.activation(out=art[:], in_=PS[7][:, :], func=AF.Sqrt,
                                                           scale=1.0 / (128 * 0.64), bias=EPS / 0.64), reads=[PB[7]], writes=[artB])
                        S.op("dve", lambda e: e.reciprocal(out=ars[:], in_=art[:]), reads=[artB], writes=[arsB])
                        S.op("dve", lambda e: e.scalar_tensor_tensor(out=mix[:, hd, qs], in0=o0[:], scalar=pkc("gattn"), in1=ars[:],
                                                                     op0=ALU.mult, op1=ALU.mult),
                             reads=[o0B, arsB, cB], writes=[mixB])
                S.barrier()
            if stop_after == "attn":
                dump("mix", mix[:], [128, DC, NR], BF16, [mixB])
                finish()
                return nc
            with contextlib.ExitStack() as ost:
                wob = [ost.enter_context(nc.sbuf_tensor(f"o_wob{i}", [128, DC, 128], BF16)) for i in range(2)]
                woB = [Buf(f"wob{i}") for i in range(2)]
                io = 0
                for dc in range(DC):
                    sl = dc % 2
                    S.dma("pool", wob[sl][:], wout[dc], writes=[woB[sl]])
                    for (t0, tn) in TCH[:2]:
                        b = io % 4; io += 1
                        for mc in range(DC):
                            S.op("pe", lambda e, mc=mc: e.matmul(PS[b][:, 0:tn], wob[sl][:, mc, :], mix[:, mc, t0:t0 + tn],
                                                                 start=(mc == 0), stop=(mc == DC - 1)),
                                 reads=[woB[sl], mixB], writes=[PB[b]])
                        S.op("dve", lambda e: e.tensor_tensor(out=h[:, dc, t0:t0 + tn], in0=PS[b][:, 0:tn], in1=h[:, dc, t0:t0 + tn], op=ALU.add),
                             reads=[PB[b], hB], writes=[hB])
                S.barrier()
            if stop_after == "outproj":
                dump("h", h[:], [128, DC, NT], F32, [hB])
                finish()
                return nc
        ffn(1, "g2", [(0, 342), (342, 342), (684, 340)])
        finish()
    return nc


def _prep(inputs):
    f32 = np.float32
    x = np.asarray(inputs["x"], f32)
    meta = np.asarray(inputs["meta_tokens"], f32)

    def gate_tiles(W):
        return np.ascontiguousarray(W.reshape(DC, 128, FC, 128).transpose(2, 1, 0, 3))

    def down_tiles(W):
        return np.ascontiguousarray(W.reshape(NQR, FQ, 128, DC, 128).transpose(0, 3, 2, 1, 4))

    shared = {}
    shared["wg1"] = gate_tiles(inputs["ffn1_w_gate"][0])
    shared["wu1"] = gate_tiles(inputs["ffn1_w_up"][0])
    shared["wd1"] = down_tiles(inputs["ffn1_w_down"][0])
    shared["wg2"] = gate_tiles(inputs["ffn2_w_gate"][0])
    shared["wu2"] = gate_tiles(inputs["ffn2_w_up"][0])
    shared["wd2"] = down_tiles(inputs["ffn2_w_down"][0])
    Win = np.asarray(inputs["w_in"][0], f32)
    cols = np.r_[0:2048, 3072:4096, 4096:5632]
    shared["winfm"] = np.ascontiguousarray(Win[:, cols].reshape(DC, 128, 36, 128).transpose(2, 1, 0, 3))
    shared["winv"] = np.ascontiguousarray(Win[:, 2048:3072].reshape(DC, 128, 1024).transpose(1, 0, 2))
    shared["windt"] = np.ascontiguousarray(Win[:, 5632:5648].reshape(DC, 128, 16).transpose(1, 0, 2))
    Wout = np.asarray(inputs["w_out"][0], f32)
    shared["wout"] = np.ascontiguousarray(Wout.reshape(DC, 128, DC, 128).transpose(2, 1, 0, 3))

    def fm(v, n):
        return np.asarray(v, f32).reshape(n, 128).T

    pkb = np.zeros((128, NPK), f32)

    def put(name, arr):
        o, w = PK[name]
        pkb[:, o:o + w] = arr

    put("g1", fm(inputs["ffn1_norm"][0], 16))
    put("gm", fm(inputs["mix_norm"][0], 16))
    put("g2", fm(inputs["ffn2_norm"][0], 16))
    put("gssd", fm(inputs["ssd_norm"][0], 8))
    put("gattn", np.asarray(inputs["attn_out_norm"][0], f32).reshape(128, 1))
    put("gq", np.tile(np.asarray(inputs["q_norm"][0], f32), 2).reshape(128, 1))
    put("gk", np.tile(np.asarray(inputs["k_norm"][0], f32), 2).reshape(128, 1))
    cw = np.asarray(inputs["conv_w"][0], f32)
    put("cw", cw.reshape(4, 12, 128).transpose(2, 1, 0).reshape(128, 48))
    put("cb", fm(inputs["conv_b"][0], 12))
    dsk = np.asarray(inputs["d_skip"][0], f32)
    put("dsk", np.repeat(dsk.reshape(8, 2), 64, axis=1).T)
    put("dtb", np.tile(np.asarray(inputs["dt_bias"][0], f32)[None], (128, 1)))
    put("alog", np.tile(np.asarray(inputs["a_log"][0], f32)[None], (128, 1)))
    lamc = np.zeros((128, 4), f32)
    for i, n in enumerate(["lambda_q1", "lambda_k1", "lambda_q2", "lambda_k2"]):
        lamc[:64, i] = np.asarray(inputs[n][0], f32)
    put("lam", lamc)

    ident = np.eye(128, dtype=f32)
    ones = np.ones((128, 128), f32)
    bd = np.zeros((128, 128), f32); bd[:64, :64] = 1; bd[64:, 64:] = 1
    rot = np.zeros((128, 128), f32)
    for blk in (0, 64):
        for d in range(8):
            rot[blk + d + 8, blk + d] = -1.0
            rot[blk + d, blk + d + 8] = 1.0
    shared["cbf"] = np.ascontiguousarray(np.stack([ident, ones, bd, rot], axis=1)).astype(NPBF)
    jj = np.arange(128)
    tle = (jj[:, None] <= jj[None, :]).astype(f32)
    tgt = (jj[:, None] > jj[None, :]).astype(f32)
    shared["cf"] = np.ascontiguousarray(np.stack([ident, ones, tle, tgt], axis=1))
    kk = np.arange(4 * NR)
    gtok = ((kk % 1024) // 256) * 1024 + (kk // 1024) * 256 + kk % 256
    shared["umask"] = (gtok[None, :] // 64 == np.arange(64)[:, None]).astype(f32).astype(NPBF)

    inv = np.power(500000.0, -np.arange(0, 16, 2, dtype=f32) / 16).astype(f32)
    in_maps = []
    for r in range(8):
        b, q = r // 4, r % 4
        real = x[b, q * NR:(q + 1) * NR]
        halo = meta[13:16] if q == 0 else x[b, q * NR - 3:q * NR]
        tok = np.concatenate([real, meta, halo, np.zeros((NT - HALO0 - 3, D), f32)], axis=0)
        m = dict(shared)
        m["xT"] = np.ascontiguousarray(tok.T.reshape(DC, 128, NT).transpose(1, 0, 2))
        pos = np.zeros(NT, f32)
        pos[:NR] = 16 + q * NR + np.arange(NR)
        pos[META0:META0 + 16] = np.arange(16)
        ang = pos[None, :] * inv[:, None]
        cosF = np.ones((128, NT), f32); sinF = np.zeros((128, NT), f32)
        for blk in (0, 64):
            cosF[blk:blk + 8] = np.cos(ang); cosF[blk + 8:blk + 16] = np.cos(ang)
            sinF[blk:blk + 8] = np.sin(ang); sinF[blk + 8:blk + 16] = np.sin(ang)
        m["cosF"] = cosF; m["sinF"] = sinF
        tq = q * 16 + np.arange(NR) // 64
        m["wmask"] = np.where(tq[None, :] < np.arange(64)[:, None], NEG, 0.0).astype(f32).astype(NPBF)
        pkr = pkb.copy()
        bm = np.zeros((4, 5), f32)
        for q2 in range(4):
            for q1 in range(4):
                bm[q2, q1] = 1.0 if (q1 < q2 < q) else 0.0
            bm[q2, 4] = 1.0 if q2 < q else 0.0
        o, w = PK["bm"]; pkr[:, o:o + w] = bm.reshape(1, 20)
        msk = np.array([1.0 if q1 < q else 0.0 for q1 in range(4)] + [1.0], f32)
        o, w = PK["msk"]; pkr[:, o:o + w] = msk[None]
        m["pk"] = pkr
        in_maps.append(m)
    return in_maps


_NC_CACHE = {}


def kernel(**inputs):
    in_maps = _prep(inputs)
    key = "full"
    if key not in _NC_CACHE:
        _NC_CACHE[key] = build_program()
    nc = _NC_CACHE[key]
    res = run_bass_kernel_spmd(nc, in_maps, core_ids=list(range(8)))
    out = np.zeros((2, 4096, D), np.float32)
    for r in range(8):
        b, q = r // 4, r % 4
        o = res.results[r]["outT"]
        out[b, q * NR:(q + 1) * NR] = o.transpose(2, 1, 0).reshape(NR, D)
    return out
```

```python
import contextlib
import math
import numpy as np
import ml_dtypes
import concourse.bass as bass
import concourse.mybir as mybir
from concourse.bass_utils import run_bass_kernel_spmd

F32 = mybir.dt.float32
BF16 = mybir.dt.bfloat16
ALU = mybir.AluOpType
AF = mybir.ActivationFunctionType
NPBF = ml_dtypes.bfloat16

D = 2048
DC = 16
DFF = 5632
FC = 44
NQR = 4
FQ = 11
NR = 1024
NT = 1048
META0 = 1024
HALO0 = 1040
TCH = [(0, 512), (512, 512), (1024, 24)]
EPS = 1e-6
NEG = -30000.0
GROUPS = [[0, 1, 2, 3], [4, 5, 6, 7]]
USE_CC = True

PK = {}
_o = 0
for _n, _w in [("g1", 16), ("gm", 16), ("g2", 16), ("gssd", 8), ("gattn", 1), ("gq", 1), ("gk", 1),
               ("cw", 48), ("cb", 12), ("dsk", 8), ("dtb", 16), ("alog", 16), ("lam", 4),
               ("bm", 20), ("msk", 5)]:
    PK[_n] = (_o, _w)
    _o += _w
NPK = _o


class Buf:
    __slots__ = ("name", "lw", "rd", "sem", "semval")

    def __init__(self, name):
        self.name = name
        self.lw = None
        self.rd = {}
        self.sem = None
        self.semval = 0


class Sched:
    ENG = ("pe", "act", "dve", "pool", "sp")

    def __init__(self, nc, stack):
        self.nc = nc
        self.stack = stack
        self.eng = {"pe": nc.tensor, "act": nc.scalar, "dve": nc.vector, "pool": nc.gpsimd, "sp": nc.sync}
        self.sem = {e: stack.enter_context(nc.semaphore(f"s_{e}")) for e in self.ENG}
        self.n = {e: 0 for e in self.ENG}
        self.last = {e: None for e in self.ENG}
        self.evs = {e: [] for e in self.ENG}
        self.cnt = {e: 0 for e in self.ENG}
        self.waited = {e: {} for e in self.ENG}
        self.dbufs = []

    def _event_for(self, e, seq):
        evs = self.evs[e]
        lo, hi = 0, len(evs)
        while lo < hi:
            mid = (lo + hi) // 2
            if evs[mid][0] >= seq:
                hi = mid
            else:
                lo = mid + 1
        if lo < len(evs):
            return (self.sem[e], evs[lo][1])
        self.cnt[e] += 1
        self.last[e].then_inc(self.sem[e], 1)
        evs.append((self.n[e] - 1, self.cnt[e]))
        return (self.sem[e], self.cnt[e])

    def _wait(self, e, tok):
        if tok is None:
            return
        if tok[0] == e and e == "pe":
            return
        if tok[0] == "dma":
            sem, val = tok[1], tok[2]
        else:
            sem, val = self._event_for(tok[0], tok[1])
        key = id(sem)
        w = self.waited[e]
        if w.get(key, 0) >= val:
            return
        w[key] = val
        self.eng[e].wait_ge(sem, val)

    def _deps(self, e, reads, writes):
        for b in reads:
            self._wait(e, b.lw)
        for b in writes:
            self._wait(e, b.lw)
            for t in b.rd.values():
                self._wait(e, t)

    @staticmethod
    def _mark(tok, key, reads, writes):
        for b in reads:
            b.rd[key] = tok
        for b in writes:
            b.lw = tok
            b.rd = {}

    def op(self, e, fn, reads=(), writes=()):
        self._deps(e, reads, writes)
        ins = fn(self.eng[e])
        seq = self.n[e]
        self.n[e] += 1
        self.last[e] = ins
        self._mark((e, seq), e, reads, writes)
        return ins

    def dmaish(self, q, fn, reads=(), writes=(), inc=16):
        self._deps(q, reads, writes)
        wb = writes[0]
        if wb.sem is None:
            wb.sem = self.stack.enter_context(self.nc.semaphore(f"d{len(self.dbufs)}_{wb.name}"))
            self.dbufs.append(wb)
        wb.semval += inc
        ins = fn(self.eng[q])
        ins.then_inc(wb.sem, inc)
        tok = ("dma", wb.sem, wb.semval)
        self._mark(tok, id(wb.sem), reads, writes)
        return ins

    def dma(self, q, out, in_, reads=(), writes=()):
        return self.dmaish(q, lambda e: e.dma_start(out=out, in_=in_), reads, writes)

    def barrier(self):
        toks = [(e2, self.n[e2] - 1) for e2 in self.ENG if self.n[e2] > 0]
        toks += [("dma", b.sem, b.semval) for b in self.dbufs]
        for e in self.ENG:
            for t in toks:
                if t[0] != e:
                    self._wait(e, t)

    def wait_all(self, e, bufs):
        for b in bufs:
            self._wait(e, b.lw)


def build_program(stop_after=None, dbg=False, skip_ffn1=False):
    nc = bass.Bass("TRN2", target_bir_lowering=False)

    def din(name, shape, dt=F32):
        return nc.dram_tensor(name, list(shape), dt, kind="ExternalInput").ap()

    def dscr(name, shape, dt=F32):
        return nc.dram_tensor(name, list(shape), dt).ap()

    xT = din("xT", [128, DC, NT])
    wg = [din("wg1", [FC, 128, DC, 128]), din("wg2", [FC, 128, DC, 128])]
    wu = [din("wu1", [FC, 128, DC, 128]), din("wu2", [FC, 128, DC, 128])]
    wd = [din("wd1", [NQR, DC, 128, FQ, 128]), din("wd2", [NQR, DC, 128, FQ, 128])]
    winfm = din("winfm", [36, 128, DC, 128])
    winv = din("winv", [128, DC, 1024])
    windt = din("windt", [128, DC, 16])
    wout = din("wout", [DC, 128, DC, 128])
    pk_d = din("pk", [128, NPK])
    cos_d = din("cosF", [128, NT])
    sin_d = din("sinF", [128, NT])
    wmask_d = din("wmask", [64, NR], BF16)
    umask_d = din("umask", [64, 4 * NR], BF16)
    cbf_d = din("cbf", [128, 4, 128], BF16)
    cf_d = din("cf", [128, 4, 128])
    outT = nc.dram_tensor("outT", [128, DC, NR], F32, kind="ExternalOutput").ap()
    dbg_d = nc.dram_tensor("dbg", [128, DC, NT], F32, kind="ExternalOutput").ap() if dbg else None

    q_d = dscr("q_d", [8, 128, NR], BF16)
    ksend = [dscr(f"ksend{j}", [NR, 256], BF16) for j in range(4)]
    kall = [dscr(f"kall{j}", [4 * NR, 256], BF16) for j in range(4)]
    vsend = [dscr(f"vsend{j}", [256, NR], BF16) for j in range(4)]
    vall = [dscr(f"vall{j}", [4 * 256, NR], BF16) for j in range(4)]
    zs_d = dscr("zs_d", [8, 128, NR], BF16)
    xfm_d = dscr("xfm_d", [8, 128, NR], F32)
    cfm_d = dscr("cfm_d", [2, 128, NR + 16], BF16)
    bfm_d = dscr("bfm_d", [2, 128, NR + 16], BF16)
    xtok_d = dscr("xtok_d", [9, 128, 1024], F32)
    btok_d = dscr("btok_d", [9, 128, 256], BF16)
    fsend = [dscr(f"fsend{j}", [128, 512], F32) for j in range(2)]
    fall = [dscr(f"fall{j}", [4 * 128, 512], F32) for j in range(2)]
    dsend = dscr("dsend", [128, 16], F32)
    dall = dscr("dall", [4 * 128, 16], F32)

    st = contextlib.ExitStack()
    with st:
        S = Sched(nc, st)

        def sb(name, shape, dt=F32):
            return st.enter_context(nc.sbuf_tensor("s_" + name, list(shape), dt))

        PS = [st.enter_context(nc.psum_tensor(f"ps{i}", [128, 512], F32)) for i in range(8)]
        PB = [Buf(f"ps{i}") for i in range(8)]

        h = sb("h", [128, DC, NT])
        hB = Buf("h")
        pk = sb("pk", [128, NPK])
        cbf = sb("cbf", [128, 4, 128], BF16)
        cf = sb("cf", [128, 4, 128])
        cB = Buf("consts")
        S.dma("sp", pk[:], pk_d[:, :], writes=[cB])
        S.dma("sp", cbf[:], cbf_d[:, :, :], writes=[cB])
        S.dma("sp", cf[:], cf_d[:, :, :], writes=[cB])
        for c in range(DC):
            S.dma("sp" if c % 2 else "act", h[:, c, :], xT[:, c, :], writes=[hB])
        ident_b, ones_b, bd64_b, rot_b = (cbf[:, i, :] for i in range(4))
        ident_f, ones_f, tle_f, tgt_f = (cf[:, i, :] for i in range(4))

        def pkc(name, i=0, n=1):
            o, w = PK[name]
            return pk[:, o + i:o + i + n]

        def ffn(li, gname, tcs):
            with contextlib.ExitStack() as fst:
                def fsb(name, shape, dt=F32):
                    return fst.enter_context(nc.sbuf_tensor(f"s_{name}_{li}", list(shape), dt))
                nonlocal_sb = fsb
                hn = fsb("hn", [128, DC, NT], BF16)
                hnB = Buf("hn")
                act = fsb("act", [128, FQ, NT], BF16)
                actB = [Buf(f"act{i}") for i in range(FQ)]
                wgb = [fsb(f"wgb{i}", [128, DC, 128], BF16) for i in range(2)]
                wub = [fsb(f"wub{i}", [128, DC, 128], BF16) for i in range(2)]
                wgB = [Buf(f"wgb{i}") for i in range(2)]
                wuB = [Buf(f"wub{i}") for i in range(2)]
                wdb = [fsb(f"wdb{i}", [128, FQ, 128], BF16) for i in range(2)]
                wdB = [Buf(f"wdb{i}") for i in range(2)]
                sg = [fsb(f"sg{i}", [128, 512]) for i in range(2)]
                sgB = [Buf(f"sg{i}") for i in range(2)]
                sq = [fsb(f"sq{i}", [128, 512], BF16) for i in range(2)]
                sqB = [Buf(f"sq{i}") for i in range(2)]
                rt = fsb("rt", [128, 512]); rtB = Buf("rt")
                rs = fsb("rs", [128, 512]); rsB = Buf("rs")
                for (t0, tn) in tcs:
                    for c in range(DC):
                        S.op("act", lambda e, c=c: e.activation(out=sq[c % 2][:, 0:tn], in_=h[:, c, t0:t0 + tn],
                                                                func=AF.Square), reads=[hB], writes=[sqB[c % 2]])
                        S.op("pe", lambda e, c=c: e.matmul(PS[7][:, 0:tn], ones_b, sq[c % 2][:, 0:tn],
                                                           start=(c == 0), stop=(c == DC - 1)),
                             reads=[sqB[c % 2], cB], writes=[PB[7]])
                    S.op("act", lambda e: e.activation(out=rt[:, 0:tn], in_=PS[7][:, 0:tn], func=AF.Sqrt,
                                                       scale=1.0 / D, bias=EPS), reads=[PB[7]], writes=[rtB])
                    S.op("dve", lambda e: e.reciprocal(out=rs[:, 0:tn], in_=rt[:, 0:tn]), reads=[rtB], writes=[rsB])
                    for c in range(DC):
                        S.op("dve", lambda e, c=c: e.scalar_tensor_tensor(
                            out=hn[:, c, t0:t0 + tn], in0=h[:, c, t0:t0 + tn], scalar=pkc(gname, c),
                            in1=rs[:, 0:tn], op0=ALU.mult, op1=ALU.mult), reads=[hB, rsB, cB], writes=[hnB])
                it = 0
                jd = 0
                for qr in range(NQR):
                    for fi in range(FQ):
                        f = qr * FQ + fi
                        sl = f % 2
                        S.dma("pool", wgb[sl][:], wg[li][f], writes=[wgB[sl]])
                        S.dma("pool", wub[sl][:], wu[li][f], writes=[wuB[sl]])
                        for (t0, tn) in tcs:
                            bg, bu = it % 2, 2 + it % 2
                            for k in range(DC):
                                S.op("pe", lambda e, k=k: e.matmul(PS[bg][:, 0:tn], wgb[sl][:, k, :], hn[:, k, t0:t0 + tn],
                                                                   start=(k == 0), stop=(k == DC - 1)),
                                     reads=[wgB[sl], hnB], writes=[PB[bg]])
                            for k in range(DC):
                                S.op("pe", lambda e, k=k: e.matmul(PS[bu][:, 0:tn], wub[sl][:, k, :], hn[:, k, t0:t0 + tn],
                                                                   start=(k == 0), stop=(k == DC - 1)),
                                     reads=[wuB[sl], hnB], writes=[PB[bu]])
                            S.op("act", lambda e: e.activation(out=sg[it % 2][:, 0:tn], in_=PS[bg][:, 0:tn], func=AF.Silu),
                                 reads=[PB[bg]], writes=[sgB[it % 2]])
                            S.op("dve", lambda e: e.tensor_tensor(out=act[:, fi, t0:t0 + tn], in0=sg[it % 2][:, 0:tn],
                                                                  in1=PS[bu][:, 0:tn], op=ALU.mult),
                                 reads=[sgB[it % 2], PB[bu]], writes=[actB[fi]])
                            it += 1
                    for dc in range(DC):
                        sl = jd % 2
                        S.dma("pool", wdb[sl][:], wd[li][qr, dc], writes=[wdB[sl]])
                        for (t0, tn) in tcs:
                            bd = 4 + jd % 2
                            for fi in range(FQ):
                                S.op("pe", lambda e, fi=fi: e.matmul(PS[bd][:, 0:tn], wdb[sl][:, fi, :], act[:, fi, t0:t0 + tn],
                                                                     start=(fi == 0), stop=(fi == FQ - 1)),
                                     reads=[wdB[sl], actB[fi]], writes=[PB[bd]])
                            S.op("dve", lambda e: e.scalar_tensor_tensor(
                                out=h[:, dc, t0:t0 + tn], in0=PS[bd][:, 0:tn], scalar=0.5, in1=h[:, dc, t0:t0 + tn],
                                op0=ALU.mult, op1=ALU.add), reads=[PB[bd], hB], writes=[hB])
                            jd += 1
                S.barrier()

        dbg_outs = {}

        def dump(name, src_ap, shape, dt, readsB):
            if not dbg:
                return
            t = nc.dram_tensor("dbg_" + name, list(shape), dt, kind="ExternalOutput").ap()
            b = Buf("dbg_" + name)
            S.dma("sp", t, src_ap, reads=readsB, writes=[b])
            dbg_outs[name] = b

        def finish():
            oB = Buf("out")
            for c in range(DC):
                S.dma("sp", outT[:, c, :], h[:, c, 0:NR], reads=[hB], writes=[oB])
            S.wait_all("sp", [oB] + list(dbg_outs.values()))

        def bc(ap2, n):
            return ap2.unsqueeze(2).to_broadcast([ap2.shape[0], ap2.shape[1], n])

        def bcm(ap2, n):
            return ap2.unsqueeze(1).to_broadcast([ap2.shape[0], n, ap2.shape[1]])

        if not skip_ffn1:
            ffn(0, "g1", [(0, 350), (350, 350), (700, 348)])
        if stop_after == "ffn1":
            dump("h", h[:], [128, DC, NT], F32, [hB])
            finish()
            return nc

        mst = contextlib.ExitStack()
        with mst:
            def msb(name, shape, dt=F32):
                return mst.enter_context(nc.sbuf_tensor("m_" + name, list(shape), dt))

            dt_sb = msb("dt", [128, 9, 16]); dtB = Buf("dt")
            dA_sb = msb("dA", [128, 9, 16]); dAB = Buf("dA")
            kmeta = msb("kmeta", [128, 8, 2, 16], BF16); kmB = Buf("kmeta")
            vmeta = msb("vmeta", [16, 1024], BF16); vmB = Buf("vmeta")
            nlam = msb("nlam", [128, 1]); nlamB = Buf("nlam")
            aneg = msb("aneg", [128, 16]); anegB = Buf("aneg")
            small = msb("small", [128, 8]); smallB = Buf("small")
            _q2 = [Buf(f"q_d{i}") for i in range(2)]; q_dB = [_q2[i % 2] for i in range(8)]
            _k2 = [[Buf(f"ksend{j}_{i}") for i in range(2)] for j in range(4)]
            ksB = [[_k2[j][i % 2] for i in range(8)] for j in range(4)]; kaB = [Buf(f"kall{j}") for j in range(4)]
            vsB = [Buf(f"vsend{t}") for t in range(8)]; vaB = [Buf(f"vall{j}") for j in range(4)]
            _z2 = [Buf(f"zs{i}") for i in range(2)]; zsB = [_z2[i % 2] for i in range(8)]
            _x2 = [Buf(f"xfm{i}") for i in range(2)]; xfmB = [_x2[i % 2] for i in range(8)]
            _t2 = [Buf(f"xtok{i}") for i in range(2)]; xtokB = [_t2[i % 2] for i in range(8)]
            cfmB = [Buf(f"cfm{i}") for i in range(2)]; bfmB = [Buf(f"bfm{i}") for i in range(2)]; btokB = [Buf(f"btok{i}") for i in range(2)]
            fsB = [Buf("fsend0"), Buf("fsend1"), Buf("dsend")]
            faB = [Buf("fall0"), Buf("fall1"), Buf("dall")]

            lo, _ = PK["lam"]
            S.op("dve", lambda e: e.tensor_tensor(out=small[:, 0:1], in0=pk[:, lo:lo + 1], in1=pk[:, lo + 1:lo + 2], op=ALU.mult),
                 reads=[cB], writes=[smallB])
            S.op("dve", lambda e: e.tensor_tensor(out=small[:, 1:2], in0=pk[:, lo + 2:lo + 3], in1=pk[:, lo + 3:lo + 4], op=ALU.mult),
                 reads=[cB], writes=[smallB])
            S.op("pe", lambda e: e.matmul(PS[7][:, 0:2], ones_f, small[:, 0:2], start=True, stop=True),
                 reads=[smallB, cB], writes=[PB[7]])
            S.op("act", lambda e: e.activation(out=small[:, 2:4], in_=PS[7][:, 0:2], func=AF.Exp), reads=[PB[7]], writes=[smallB])
            S.op("dve", lambda e: e.tensor_tensor(out=small[:, 4:5], in0=small[:, 3:4], in1=small[:, 2:3], op=ALU.subtract),
                 reads=[smallB], writes=[smallB])
            S.op("dve", lambda e: e.tensor_scalar(out=nlam[:, 0:1], in0=small[:, 4:5], scalar1=-0.2, scalar2=None, op0=ALU.add),
                 reads=[smallB], writes=[nlamB])
            ao, _ = PK["alog"]
            S.op("act", lambda e: e.activation(out=aneg[:], in_=pk[:, ao:ao + 16], func=AF.Exp), reads=[cB], writes=[anegB])
            S.op("dve", lambda e: e.tensor_scalar(out=aneg[:], in0=aneg[:], scalar1=-1.0, scalar2=None, op0=ALU.mult),
                 reads=[anegB], writes=[anegB])

            with contextlib.ExitStack() as ist:
                def isb(name, shape, dt=F32):
                    return ist.enter_context(nc.sbuf_tensor("i_" + name, list(shape), dt))
                hn = isb("hn", [128, DC, NT], BF16); hnB = Buf("ihn")
                wvb = isb("wvb", [128, DC, 1024], BF16); wvB = Buf("wvb")
                wdtb = isb("wdtb", [128, DC, 16], BF16); wdtB = Buf("wdtb")
                wt = [isb(f"wt{i}", [128, DC, 128], BF16) for i in range(2)]
                wtB = [Buf(f"wt{i}") for i in range(2)]
                cosT = isb("cosT", [128, NT]); sinT = isb("sinT", [128, NT]); csB = Buf("cs")
                sq = [isb(f"sq{i}", [128, 512], BF16) for i in range(2)]
                sqB = [Buf(f"isq{i}") for i in range(2)]
                lt2 = [isb(f"lt2{i}", [128, 512]) for i in range(2)]; lt2B = [Buf(f"lt2{i}") for i in range(2)]
                rs2 = [isb(f"rs2{i}", [128, 512]) for i in range(2)]; rs2B = [Buf(f"rs2{i}") for i in range(2)]
                rt, rtB, rs, rsB = lt2[0], lt2B[0], rs2[0], rs2B[0]
                qn32 = [isb(f"qn32{i}", [128, 512]) for i in range(2)]; qn32B = [Buf(f"qn32{i}") for i in range(2)]
                qnb = [isb(f"qnb{i}", [128, 512], BF16) for i in range(2)]; qnbB = [Buf(f"qnb{i}") for i in range(2)]
                t1 = [isb(f"t1{i}", [128, 512]) for i in range(2)]; t1B = [Buf(f"t1{i}") for i in range(2)]
                t2 = [isb(f"t2{i}", [128, 512]) for i in range(2)]; t2B = [Buf(f"t2{i}") for i in range(2)]
                qr = [isb(f"qr{i}", [128, 512], BF16) for i in range(2)]
                qrB = [Buf(f"qr{i}") for i in range(2)]
                vt = [isb(f"vt{i}", [128, 1024], BF16) for i in range(2)]
                vtB = [Buf(f"vt{i}") for i in range(2)]
                dtt = isb("dtt", [128, 16]); dttB = Buf("dtt")
                convin = isb("convin", [128, 3 + NR]); cinB = Buf("convin")
                convm = isb("convm", [128, 3 + 16]); cmB = Buf("convm")
                acc0 = isb("acc0", [128, NR]); acc = [acc0, acc0]
                accB0 = Buf("acc0"); accB = [accB0, accB0]
                accm0 = isb("accm0", [128, 16]); accm = [accm0, accm0]
                accmB0 = Buf("accm0"); accmB = [accmB0, accmB0]
                xc = isb("xc", [128, NR + 16]); xcB = Buf("xc")
                xcb = isb("xcb", [128, NR + 16], BF16); xcbB = Buf("xcb")
                xtk = isb("xtk", [128, 9, 128]); xtkB = Buf("xtk")
                btk = isb("btk", [128, 9, 128], BF16); btkB = Buf("btk")

                S.dma("pool", wvb[:], winv[:, :, :], writes=[wvB])
                S.dma("pool", wdtb[:], windt[:, :, :], writes=[wdtB])
                S.dma("sp", cosT[:], cos_d[:, :], writes=[csB])
                S.dma("sp", sinT[:], sin_d[:, :], writes=[csB])
                S.op("dve", lambda e: e.memset(convm[:], 0.0), writes=[cmB])
                S.op("dve", lambda e: e.memset(kmeta[:], 0.0), writes=[kmB])

                for (t0, tn) in TCH:
                    for c in range(DC):
                        S.op("act", lambda e, c=c: e.activation(out=sq[c % 2][:, 0:tn], in_=h[:, c, t0:t0 + tn], func=AF.Square),
                             reads=[hB], writes=[sqB[c % 2]])
                        S.op("pe", lambda e, c=c: e.matmul(PS[7][:, 0:tn], ones_b, sq[c % 2][:, 0:tn],
                                                           start=(c == 0), stop=(c == DC - 1)),
                             reads=[sqB[c % 2], cB], writes=[PB[7]])
                    S.op("act", lambda e: e.activation(out=rt[:, 0:tn], in_=PS[7][:, 0:tn], func=AF.Sqrt, scale=1.0 / D, bias=EPS),
                         reads=[PB[7]], writes=[rtB])
                    S.op("dve", lambda e: e.reciprocal(out=rs[:, 0:tn], in_=rt[:, 0:tn]), reads=[rtB], writes=[rsB])
                    for c in range(DC):
                        S.op("dve", lambda e, c=c: e.scalar_tensor_tensor(
                            out=hn[:, c, t0:t0 + tn], in0=h[:, c, t0:t0 + tn], scalar=pkc("gm", c),
                            in1=rs[:, 0:tn], op0=ALU.mult, op1=ALU.mult), reads=[hB, rsB, cB], writes=[hnB])

                dbo, _ = PK["dtb"]
                ib = 0
                for tb in range(9):
                    t0 = tb * 128
                    m = 128 if tb < 8 else 16
                    vsl = tb % 2
                    for half in range(2):
                        ba = ib % 4; ib += 1
                        for k in range(DC):
                            S.op("pe", lambda e, k=k: e.matmul(PS[ba][0:m, :], hn[:, k, t0:t0 + m], wvb[:, k, half * 512:(half + 1) * 512],
                                                               start=(k == 0), stop=(k == DC - 1)),
                                 reads=[hnB, wvB], writes=[PB[ba]])
                        if tb < 8:
                            S.op("act", lambda e: e.copy(out=vt[vsl][:, half * 512:(half + 1) * 512], in_=PS[ba][:, :]),
                                 reads=[PB[ba]], writes=[vtB[vsl]])
                        else:
                            S.op("act", lambda e: e.copy(out=vmeta[0:16, half * 512:(half + 1) * 512], in_=PS[ba][0:16, :]),
                                 reads=[PB[ba]], writes=[vmB])
                    if tb < 8:
                        S.dma("sp", vsend[tb // 2][(tb % 2) * 128:(tb % 2) * 128 + 128, :], vt[vsl][:], reads=[vtB[vsl]], writes=[vsB[tb]])
                    for k in range(DC):
                        S.op("pe", lambda e, k=k: e.matmul(PS[7][0:m, 0:16], hn[:, k, t0:t0 + m], wdtb[:, k, :],
                                                           start=(k == 0), stop=(k == DC - 1)),
                             reads=[hnB, wdtB], writes=[PB[7]])
                    S.op("dve", lambda e: e.tensor_tensor(out=dtt[0:m, :], in0=PS[7][0:m, 0:16], in1=pk[0:m, dbo:dbo + 16], op=ALU.add),
                         reads=[PB[7], cB], writes=[dttB])
                    S.op("act", lambda e: e.activation(out=dtt[0:m, :], in_=dtt[0:m, :], func=AF.Exp), reads=[dttB], writes=[dttB])
                    S.op("act", lambda e: e.activation(out=dt_sb[0:m, tb, :], in_=dtt[0:m, :], func=AF.Ln, bias=1.0, scale=1.0),
                         reads=[dttB], writes=[dtB])
                    S.op("dve", lambda e: e.tensor_tensor(out=dA_sb[0:m, tb, :], in0=dt_sb[0:m, tb, :], in1=aneg[0:m, :], op=ALU.mult),
                         reads=[dtB, anegB], writes=[dAB])

                ia = 0
                for i in range(36):
                    kind = "q" if i < 8 else "k" if i < 16 else "z" if i < 24 else "x" if i < 32 else "B" if i < 34 else "C"
                    sl = i % 2
                    S.dma("pool", wt[sl][:], winfm[i], writes=[wtB[sl]])
                    tcs = TCH[:2] if kind in ("q", "z") else TCH
                    for (t0, tn) in tcs:
                        ba = ia % 4; ia += 1
                        for k in range(DC):
                            S.op("pe", lambda e, k=k: e.matmul(PS[ba][:, 0:tn], wt[sl][:, k, :], hn[:, k, t0:t0 + tn],
                                                               start=(k == 0), stop=(k == DC - 1)),
                                 reads=[wtB[sl], hnB], writes=[PB[ba]])
                        if kind in ("q", "k"):
                            hd = i % 8
                            gname = "gq" if kind == "q" else "gk"
                            u2 = ia % 2
                            bss, brot = 4 + u2, 6 + u2
                            S.op("act", lambda e: e.activation(out=sq[u2][:, 0:tn], in_=PS[ba][:, 0:tn], func=AF.Square),
                                 reads=[PB[ba]], writes=[sqB[u2]])
                            S.op("pe", lambda e: e.matmul(PS[bss][:, 0:tn], bd64_b, sq[u2][:, 0:tn], start=True, stop=True),
                                 reads=[sqB[u2], cB], writes=[PB[bss]])
                            S.op("act", lambda e: e.activation(out=lt2[u2][:, 0:tn], in_=PS[bss][:, 0:tn], func=AF.Ln, scale=1.0 / 64, bias=EPS),
                                 reads=[PB[bss]], writes=[lt2B[u2]])
                            S.op("act", lambda e: e.activation(out=rs2[u2][:, 0:tn], in_=lt2[u2][:, 0:tn], func=AF.Exp, scale=-0.5),
                                 reads=[lt2B[u2]], writes=[rs2B[u2]])
                            S.op("dve", lambda e: e.scalar_tensor_tensor(out=qn32[u2][:, 0:tn], in0=PS[ba][:, 0:tn], scalar=pkc(gname),
                                                                         in1=rs2[u2][:, 0:tn], op0=ALU.mult, op1=ALU.mult),
                                 reads=[PB[ba], rs2B[u2], cB], writes=[qn32B[u2]])
                            S.op("act", lambda e: e.copy(out=qnb[u2][:, 0:tn], in_=qn32[u2][:, 0:tn]), reads=[qn32B[u2]], writes=[qnbB[u2]])
                            S.op("pe", lambda e: e.matmul(PS[brot][:, 0:tn], rot_b, qnb[u2][:, 0:tn], start=True, stop=True),
                                 reads=[qnbB[u2], cB], writes=[PB[brot]])
                            S.op("dve", lambda e: e.tensor_tensor(out=t1[u2][:, 0:tn], in0=qn32[u2][:, 0:tn], in1=cosT[:, t0:t0 + tn], op=ALU.mult),
                                 reads=[qn32B[u2], csB], writes=[t1B[u2]])
                            S.op("dve", lambda e: e.tensor_tensor(out=t2[u2][:, 0:tn], in0=PS[brot][:, 0:tn], in1=sinT[:, t0:t0 + tn], op=ALU.mult),
                                 reads=[PB[brot], csB], writes=[t2B[u2]])
                            qs = ia % 2
                            S.op("dve", lambda e: e.tensor_tensor(out=qr[qs][:, 0:tn], in0=t1[u2][:, 0:tn], in1=t2[u2][:, 0:tn], op=ALU.add),
                                 reads=[t1B[u2], t2B[u2]], writes=[qrB[qs]])
                            if kind == "q":
                                S.dma("sp", q_d[hd, :, t0:t0 + tn], qr[qs][:, 0:tn], reads=[qrB[qs]], writes=[q_dB[hd]])
                            elif t0 < NR:
                                for jj in range(2):
                                    jq = t0 // 256 + jj
                                    S.dma("sp", ksend[jq][hd * 128:(hd + 1) * 128, :], qr[qs][:, jj * 256:(jj + 1) * 256],
                                          reads=[qrB[qs]], writes=[ksB[jq][hd]])
                            else:
                                S.op("act", lambda e: e.copy(out=kmeta[0:64, hd, 0, :], in_=qr[qs][0:64, 0:16]), reads=[qrB[qs]], writes=[kmB])
                                S.op("act", lambda e: e.copy(out=kmeta[64:128, hd, 1, :], in_=qr[qs][64:128, 0:16]), reads=[qrB[qs]], writes=[kmB])
                        elif kind == "z":
                            qs = ia % 2
                            S.op("act", lambda e: e.activation(out=qr[qs][:, 0:tn], in_=PS[ba][:, 0:tn], func=AF.Silu),
                                 reads=[PB[ba]], writes=[qrB[qs]])
                            S.dma("sp", zs_d[i - 16, :, t0:t0 + tn], qr[qs][:, 0:tn], reads=[qrB[qs]], writes=[zsB[i - 16]])
                        else:
                            if t0 < NR:
                                S.op("act", lambda e: e.copy(out=convin[:, 3 + t0:3 + t0 + tn], in_=PS[ba][:, 0:tn]),
                                     reads=[PB[ba]], writes=[cinB])
                            else:
                                S.op("act", lambda e: e.copy(out=convin[:, 0:3], in_=PS[ba][:, 16:19]), reads=[PB[ba]], writes=[cinB])
                                S.op("act", lambda e: e.copy(out=convm[:, 3:19], in_=PS[ba][:, 0:16]), reads=[PB[ba]], writes=[cmB])
                    if kind in ("x", "B", "C"):
                        ci = i - 24
                        cwo, _ = PK["cw"]
                        cbo, _ = PK["cb"]
                        for (src, srcB, ac, acB, n, o0) in ((convin, cinB, acc, accB, NR, 0), (convm, cmB, accm, accmB, 16, NR)):
                            if kind == "C" and n == 16:
                                continue
                            S.op("dve", lambda e: e.tensor_scalar(out=ac[0][:, 0:n], in0=src[:, 0:n], scalar1=pk[:, cwo + ci * 4:cwo + ci * 4 + 1],
                                                                  scalar2=None, op0=ALU.mult), reads=[srcB, cB], writes=[acB[0]])
                            for j in range(1, 4):
                                S.op("dve", lambda e, j=j: e.scalar_tensor_tensor(
                                    out=ac[j % 2][:, 0:n], in0=src[:, j:j + n], scalar=pk[:, cwo + ci * 4 + j:cwo + ci * 4 + j + 1],
                                    in1=ac[(j - 1) % 2][:, 0:n], op0=ALU.mult, op1=ALU.add),
                                    reads=[srcB, cB, acB[(j - 1) % 2]], writes=[acB[j % 2]])
                            S.op("act", lambda e: e.activation(out=xc[:, o0:o0 + n], in_=ac[1][:, 0:n], func=AF.Silu,
                                                               bias=pk[:, cbo + ci:cbo + ci + 1], scale=1.0),
                                 reads=[acB[1], cB], writes=[xcB])
                        nv = NR + 16 if kind != "C" else NR
                        if kind == "x":
                            j = i - 24
                            S.dma("sp", xfm_d[j, :, :], xc[:, 0:NR], reads=[xcB], writes=[xfmB[j]])
                            for gi, blks in enumerate(((0, 1, 2, 3), (4, 5, 6, 7), (8,))):
                                bt = 4 + gi
                                w = 128 if gi < 2 else 16
                                for kk_, blk in enumerate(blks):
                                    S.op("pe", lambda e: e.transpose(PS[bt][0:w, kk_ * 128:(kk_ + 1) * 128], xc[:, blk * 128:blk * 128 + w], ident_f),
                                         reads=[xcB, cB], writes=[PB[bt]])
                                nb = len(blks)
                                S.op("act", lambda e: e.copy(out=xtk[0:w, blks[0]:blks[0] + nb, :],
                                                             in_=PS[bt][0:w, 0:nb * 128].rearrange("p (b f) -> p b f", b=nb)),
                                     reads=[PB[bt]], writes=[xtkB])
                            S.dma("sp", xtok_d[0:8, :, j * 128:(j + 1) * 128].rearrange("b p f -> p b f"), xtk[:, 0:8, :],
                                  reads=[xtkB], writes=[xtokB[j]])
                            S.dma("sp", xtok_d[8, 0:16, j * 128:(j + 1) * 128], xtk[0:16, 8, :], reads=[xtkB], writes=[xtokB[j]])
                        else:
                            g = (i - 32) % 2
                            S.op("act", lambda e: e.copy(out=xcb[:, 0:nv], in_=xc[:, 0:nv]), reads=[xcB], writes=[xcbB])
                            if kind == "B":
                                S.dma("sp", bfm_d[g, :, :], xcb[:, :], reads=[xcbB], writes=[bfmB[g]])
                                for gi, blks in enumerate(((0, 1, 2, 3), (4, 5, 6, 7), (8,))):
                                    bt = 4 + gi
                                    w = 128 if gi < 2 else 16
                                    for kk_, blk in enumerate(blks):
                                        S.op("pe", lambda e: e.transpose(PS[bt][0:w, kk_ * 128:(kk_ + 1) * 128], xc[:, blk * 128:blk * 128 + w], ident_f),
                                             reads=[xcB, cB], writes=[PB[bt]])
                                    nb = len(blks)
                                    S.op("act", lambda e: e.copy(out=btk[0:w, blks[0]:blks[0] + nb, :],
                                                                 in_=PS[bt][0:w, 0:nb * 128].rearrange("p (b f) -> p b f", b=nb)),
                                         reads=[PB[bt]], writes=[btkB])
                                S.dma("sp", btok_d[0:8, :, g * 128:(g + 1) * 128].rearrange("b p f -> p b f"), btk[:, 0:8, :],
                                      reads=[btkB], writes=[btokB[g]])
                                S.dma("sp", btok_d[8, 0:16, g * 128:(g + 1) * 128], btk[0:16, 8, :], reads=[btkB], writes=[btokB[g]])
                            else:
                                S.dma("sp", cfm_d[g, :, 0:NR], xcb[:, 0:NR], reads=[xcbB], writes=[cfmB[g]])

                if USE_CC:
                    for j in range(4):
                        S.dmaish("pool", lambda e, j=j: e.collective_compute("AllGather", ALU.bypass, replica_groups=GROUPS,
                                                                              ins=[ksend[j][:, :]], outs=[kall[j][:, :]]),
                                 reads=ksB[j], writes=[kaB[j]], inc=1)
                        S.dmaish("pool", lambda e, j=j: e.collective_compute("AllGather", ALU.bypass, replica_groups=GROUPS,
                                                                              ins=[vsend[j][:, :]], outs=[vall[j][:, :]]),
                                 reads=[vsB[2 * j], vsB[2 * j + 1]], writes=[vaB[j]], inc=1)
                if stop_after == "inproj":
                    dump("q", q_d, [8, 128, NR], BF16, q_dB)
                    if USE_CC:
                        dump("kall0", kall[0], [4 * NR, 256], BF16, [kaB[0]])
                        dump("vall3", vall[3], [4 * 256, NR], BF16, [vaB[3]])
                    dump("zs", zs_d, [8, 128, NR], BF16, zsB)
                    dump("xfm", xfm_d, [8, 128, NR], F32, xfmB)
                    dump("xtok", xtok_d, [9, 128, 1024], F32, xtokB)
                    dump("btok", btok_d, [9, 128, 256], BF16, btokB)
                    dump("bfm", bfm_d, [2, 128, NR + 16], BF16, bfmB)
                    dump("cfm", cfm_d, [2, 128, NR + 16], BF16, cfmB)
                    dump("dt", dt_sb[:], [128, 9, 16], F32, [dtB])
                    dump("kmeta", kmeta[:], [128, 8, 2, 16], BF16, [kmB])
                    dump("vmeta", vmeta[:], [16, 1024], BF16, [vmB])
                    finish()
                    return nc
                S.barrier()
            mix = msb("mix", [128, DC, NR], BF16)
            mixB = Buf("mix")
            with contextlib.ExitStack() as sst:
                def ssb(name, shape, dt=F32):
                    return sst.enter_context(nc.sbuf_tensor("d_" + name, list(shape), dt))
                Sloc = ssb("Sloc", [128, 9, 1024], BF16); SlocB = [Buf(f"Sloc{c}") for c in range(9)]
                Xdt = ssb("Xdt", [128, 8, 1024], BF16); XdtB = [Buf(f"Xdt{c}") for c in range(8)]
                tot_sb = ssb("tot", [128, 9, 16]); totB = Buf("tot")
                dec = ssb("dec", [128, 9, 16]); decB = Buf("dec")
                acs_t = ssb("acs", [128, 9, 16]); acsB = Buf("acs")
                yg = ssb("yg", [128, 8, 128]); ygB = Buf("yg")
                Bf = ssb("Bf", [128, 2, NR + 16], BF16); BfB = Buf("Bf")
                Cf = ssb("Cf", [128, 2, NR + 16], BF16); CfB = Buf("Cf")
                Xt = [ssb(f"Xt{i}", [128, 1024]) for i in range(2)]; XtB = [Buf(f"Xt{i}") for i in range(2)]
                Bt = [ssb(f"Bt{i}", [128, 256], BF16) for i in range(2)]; BtB = [Buf(f"Bt{i}") for i in range(2)]
                Xw = ssb("Xw", [128, 1024], BF16); XwB = Buf("Xw")
                sm = ssb("sm", [128, 4, 16]); smB = Buf("sm")
                Fst = ssb("Fst", [128, 1024]); FstB = Buf("Fst")
                Sbf = ssb("Sbf", [128, 1024], BF16); SbfB = Buf("Sbf")
                Dl = ssb("Dl", [128, 4, 16]); DlB = Buf("Dl")
                coef = ssb("coef", [128, 5, 16]); coefB = Buf("coef")
                Rr = ssb("Rr", [128, 16, 128]); RrB = Buf("Rr")
                Lx = ssb("Lx", [128, 1024]); LxB = Buf("Lx")
                Ex = ssb("Ex", [128, 1024]); ExB = Buf("Ex")
                CBm = ssb("CBm", [128, 128]); CBmB = Buf("CBm")
                MT = ssb("MT", [128, 8, 128], BF16); MTB = Buf("MT")
                Cs = ssb("Cs", [128, 8, 128], BF16); CsB = Buf("Cs")
                xfc = ssb("xfc", [128, 8, 128]); xfcB = Buf("xfc")
                zsc = ssb("zsc", [128, 8, 128], BF16); zscB = Buf("zsc")
                y1 = ssb("y1", [128, 128]); y1B = Buf("y1")
                ssq = ssb("ssq", [128, 1024], BF16); ssqB = Buf("ssq")
                srt = ssb("srt", [128, 128]); srtB = Buf("srt")
                srs = ssb("srs", [128, 128]); srsB = Buf("srs")

                S.dma("sp", Bf[:], bfm_d.rearrange("g p t -> p g t"), reads=bfmB, writes=[BfB])
                S.dma("sp", Cf[:, :, 0:NR], cfm_d.rearrange("g p t -> p g t")[:, :, 0:NR], reads=cfmB, writes=[CfB])

                for c in (8, 0, 1, 2, 3, 4, 5, 6, 7):
                    n = 128 if c < 8 else 16
                    sl = c % 2
                    S.dma("sp", Xt[sl][0:n, :], xtok_d[c, 0:n, :], reads=xtokB, writes=[XtB[sl]])
                    S.dma("act", Bt[sl][0:n, :], btok_d[c, 0:n, :], reads=btokB, writes=[BtB[sl]])
                    S.op("pe", lambda e: e.matmul(PS[7][0:n, 0:16], tle_f[0:n, 0:n], dA_sb[0:n, c, :], start=True, stop=True),
                         reads=[cB, dAB], writes=[PB[7]])
                    S.op("pe", lambda e: e.matmul(PS[7][:, 16:32], ones_f[0:n, :], dA_sb[0:n, c, :], start=True, stop=True),
                         reads=[cB, dAB], writes=[PB[7]])
                    S.op("act", lambda e: e.copy(out=acs_t[0:n, c, :], in_=PS[7][0:n, 0:16]), reads=[PB[7]], writes=[acsB])
                    S.op("act", lambda e: e.copy(out=tot_sb[:, c, :], in_=PS[7][:, 16:32]), reads=[PB[7]], writes=[totB])
                    S.op("act", lambda e: e.activation(out=dec[:, c, :], in_=PS[7][:, 16:32], func=AF.Exp), reads=[PB[7]], writes=[decB])
                    S.op("dve", lambda e: e.tensor_tensor(out=sm[0:n, 0, :], in0=PS[7][0:n, 16:32], in1=acs_t[0:n, c, :], op=ALU.subtract),
                         reads=[PB[7], acsB], writes=[smB])
                    S.op("act", lambda e: e.activation(out=sm[0:n, 1, :], in_=sm[0:n, 0, :], func=AF.Exp), reads=[smB], writes=[smB])
                    S.op("dve", lambda e: e.tensor_tensor(out=sm[0:n, 2, :], in0=sm[0:n, 1, :], in1=dt_sb[0:n, c, :], op=ALU.mult),
                         reads=[smB, dtB], writes=[smB])
                    S.op("dve", lambda e: e.tensor_tensor(out=Xw[0:n, :].rearrange("p (h d) -> p h d", h=16),
                                                          in0=Xt[sl][0:n, :].rearrange("p (h d) -> p h d", h=16),
                                                          in1=bc(sm[0:n, 2, :], 64), op=ALU.mult),
                         reads=[XtB[sl], smB], writes=[XwB])
                    if c < 8:
                        S.op("dve", lambda e: e.tensor_tensor(out=Xdt[:, c, :].rearrange("p (h d) -> p h d", h=16),
                                                              in0=Xt[sl][:, :].rearrange("p (h d) -> p h d", h=16),
                                                              in1=bc(dt_sb[:, c, :], 64), op=ALU.mult),
                             reads=[XtB[sl], dtB], writes=[XdtB[c]])
                    for g in range(2):
                        S.op("pe", lambda e: e.matmul(PS[g][:, :], Bt[sl][0:n, g * 128:(g + 1) * 128], Xw[0:n, g * 512:(g + 1) * 512],
                                                      start=True, stop=True), reads=[BtB[sl], XwB], writes=[PB[g]])
                        S.op("act", lambda e: e.copy(out=Sloc[:, c, g * 512:(g + 1) * 512], in_=PS[g][:, :]),
                             reads=[PB[g]], writes=[SlocB[c]])
                S.op("dve", lambda e: e.tensor_copy(out=Fst[:], in_=Sloc[:, 0, :]), reads=[SlocB[0]], writes=[FstB])
                for c in range(1, 8):
                    S.op("dve", lambda e: e.tensor_tensor(out=Fst[:].rearrange("p (h d) -> p h d", h=16),
                                                          in0=Fst[:].rearrange("p (h d) -> p h d", h=16),
                                                          in1=bc(dec[:, c, :], 64), op=ALU.mult), reads=[FstB, decB], writes=[FstB])
                    S.op("dve", lambda e: e.tensor_tensor(out=Fst[:], in0=Fst[:], in1=Sloc[:, c, :], op=ALU.add),
                         reads=[FstB, SlocB[c]], writes=[FstB])
                S.op("dve", lambda e: e.tensor_tensor(out=sm[:, 3, :], in0=tot_sb[:, 0, :], in1=tot_sb[:, 1, :], op=ALU.add),
                     reads=[totB], writes=[smB])
                for c in range(2, 8):
                    S.op("dve", lambda e: e.tensor_tensor(out=sm[:, 3, :], in0=sm[:, 3, :], in1=tot_sb[:, c, :], op=ALU.add),
                         reads=[totB, smB], writes=[smB])
                S.dma("sp", fsend[0][:, :], Fst[:, 0:512], reads=[FstB], writes=[fsB[0]])
                S.dma("sp", fsend[1][:, :], Fst[:, 512:1024], reads=[FstB], writes=[fsB[1]])
                S.dma("sp", dsend[:, :], sm[:, 3, :], reads=[smB], writes=[fsB[2]])
                if USE_CC:
                    for j, (snd_, rcv_) in enumerate(((fsend[0], fall[0]), (fsend[1], fall[1]), (dsend, dall))):
                        S.dmaish("pool", lambda e, snd_=snd_, rcv_=rcv_: e.collective_compute(
                            "AllGather", ALU.bypass, replica_groups=GROUPS, ins=[snd_[:, :]], outs=[rcv_[:, :]]),
                            reads=[fsB[j]], writes=[faB[j]], inc=1)
                S.dma("sp", Dl[:], dall.rearrange("(r p) f -> p r f", p=128), reads=[faB[2]], writes=[DlB])
                bmo, _ = PK["bm"]
                mko, _ = PK["msk"]
                for q1 in range(5):
                    S.op("dve", lambda e: e.tensor_scalar(out=coef[:, q1, :], in0=Dl[:, 0, :], scalar1=pk[:, bmo + q1:bmo + q1 + 1],
                                                          scalar2=None, op0=ALU.mult), reads=[DlB, cB], writes=[coefB])
                    for q2 in range(1, 4):
                        S.op("dve", lambda e, q2=q2: e.scalar_tensor_tensor(
                            out=coef[:, q1, :], in0=Dl[:, q2, :], scalar=pk[:, bmo + q2 * 5 + q1:bmo + q2 * 5 + q1 + 1],
                            in1=coef[:, q1, :], op0=ALU.mult, op1=ALU.add), reads=[DlB, cB, coefB], writes=[coefB])
                    S.op("act", lambda e: e.activation(out=coef[:, q1, :], in_=coef[:, q1, :], func=AF.Exp), reads=[coefB], writes=[coefB])
                    S.op("dve", lambda e: e.tensor_scalar(out=coef[:, q1, :], in0=coef[:, q1, :], scalar1=pk[:, mko + q1:mko + q1 + 1],
                                                          scalar2=None, op0=ALU.mult), reads=[coefB, cB], writes=[coefB])
                S.op("dve", lambda e: e.tensor_tensor(out=Fst[:].rearrange("p (h d) -> p h d", h=16),
                                                      in0=Sloc[:, 8, :].rearrange("p (h d) -> p h d", h=16),
                                                      in1=bc(coef[:, 4, :], 64), op=ALU.mult), reads=[SlocB[8], coefB], writes=[FstB])
                for q1 in range(4):
                    S.dma("sp", Ex[:, 0:512], fall[0][q1 * 128:(q1 + 1) * 128, :], reads=[faB[0]], writes=[ExB])
                    S.dma("act", Ex[:, 512:1024], fall[1][q1 * 128:(q1 + 1) * 128, :], reads=[faB[1]], writes=[ExB])
                    S.op("dve", lambda e: e.tensor_tensor(out=Lx[:].rearrange("p (h d) -> p h d", h=16),
                                                          in0=Ex[:].rearrange("p (h d) -> p h d", h=16),
                                                          in1=bc(coef[:, q1, :], 64), op=ALU.mult), reads=[ExB, coefB], writes=[LxB])
                    S.op("dve", lambda e: e.tensor_tensor(out=Fst[:], in0=Fst[:], in1=Lx[:], op=ALU.add), reads=[FstB, LxB], writes=[FstB])

                dso, _ = PK["dsk"]
                for c in range(8):
                    cs = slice(c * 128, (c + 1) * 128)
                    S.op("act", lambda e: e.copy(out=Sbf[:], in_=Fst[:]), reads=[FstB], writes=[SbfB])
                    S.op("dve", lambda e: e.tensor_tensor(out=Rr[:], in0=bc(dA_sb[:, c, :], 128), in1=bcm(tle_f, 16), op=ALU.mult),
                         reads=[dAB, cB], writes=[RrB])
                    S.dma("sp", xfc[:], xfm_d[:, :, cs].rearrange("i p t -> p i t"), reads=xfmB, writes=[xfcB])
                    S.dma("act", zsc[:], zs_d[:, :, cs].rearrange("i p t -> p i t"), reads=zsB, writes=[zscB])
                    for g in range(2):
                        rr = Rr[:, 8 * g:8 * g + 8, :].rearrange("p h l -> p (h l)")
                        for hf in range(2):
                            S.op("pe", lambda e: e.matmul(PS[hf][:, :], tgt_f, rr[:, hf * 512:(hf + 1) * 512], start=True, stop=True),
                                 reads=[cB, RrB], writes=[PB[hf]])
                            S.op("pe", lambda e: e.matmul(PS[2 + hf][:, :], ones_f, rr[:, hf * 512:(hf + 1) * 512], start=True, stop=True),
                                 reads=[cB, RrB], writes=[PB[2 + hf]])
                            S.op("act", lambda e: e.activation(out=Lx[:, hf * 512:(hf + 1) * 512], in_=PS[hf][:, :], func=AF.Exp),
                                 reads=[PB[hf]], writes=[LxB])
                            S.op("act", lambda e: e.activation(out=Ex[:, hf * 512:(hf + 1) * 512], in_=PS[2 + hf][:, :], func=AF.Exp),
                                 reads=[PB[2 + hf]], writes=[ExB])
                        S.op("pe", lambda e: e.matmul(PS[4][:, 0:128], Bf[:, g, cs], Cf[:, g, cs], start=True, stop=True),
                             reads=[BfB, CfB], writes=[PB[4]])
                        S.op("dve", lambda e: e.tensor_tensor(out=CBm[:], in0=PS[4][:, 0:128], in1=tle_f, op=ALU.mult),
                             reads=[PB[4], cB], writes=[CBmB])
                        S.op("dve", lambda e: e.tensor_tensor(out=MT[:], in0=Lx[:].rearrange("p (h l) -> p h l", h=8),
                                                              in1=bcm(CBm[:], 8), op=ALU.mult), reads=[LxB, CBmB], writes=[MTB])
                        S.op("dve", lambda e: e.tensor_tensor(out=Cs[:], in0=Ex[:].rearrange("p (h l) -> p h l", h=8),
                                                              in1=bcm(Cf[:, g, cs], 8), op=ALU.mult), reads=[ExB, CfB], writes=[CsB])
                        yb = 5 + g
                        for hh in range(8):
                            hd = 8 * g + hh
                            half = hd % 2
                            ii = hh // 2
                            oap = PS[yb][half * 64:(half + 1) * 64, ii * 128:(ii + 1) * 128]
                            S.op("pe", lambda e: e.matmul(oap, Xdt[:, c, hd * 64:(hd + 1) * 64], MT[:, hh, :], start=True, stop=False),
                                 reads=[XdtB[c], MTB], writes=[PB[yb]])
                            S.op("pe", lambda e: e.matmul(oap, Sbf[:, hd * 64:(hd + 1) * 64], Cs[:, hh, :], start=False, stop=True),
                                 reads=[SbfB, CsB], writes=[PB[yb]])
                        for ii in range(4):
                            i = 4 * g + ii
                            S.op("dve", lambda e: e.scalar_tensor_tensor(out=y1[:], in0=xfc[:, i, :], scalar=pk[:, dso + i:dso + i + 1],
                                                                         in1=PS[yb][:, ii * 128:(ii + 1) * 128], op0=ALU.mult, op1=ALU.add),
                                 reads=[xfcB, cB, PB[yb]], writes=[y1B])
                            S.op("dve", lambda e: e.tensor_tensor(out=yg[:, i, :], in0=y1[:], in1=zsc[:, i, :], op=ALU.mult),
                                 reads=[y1B, zscB], writes=[ygB])
                    S.op("act", lambda e: e.activation(out=ssq[:], in_=yg[:].rearrange("p i t -> p (i t)"), func=AF.Square),
                         reads=[ygB], writes=[ssqB])
                    for i in range(8):
                        S.op("pe", lambda e, i=i: e.matmul(PS[7][:, 0:128], ones_b, ssq[:, i * 128:(i + 1) * 128], start=(i == 0), stop=(i == 7)),
                             reads=[ssqB, cB], writes=[PB[7]])
                    S.op("act", lambda e: e.activation(out=srt[:], in_=PS[7][:, 0:128], func=AF.Sqrt, scale=1.0 / 1024, bias=EPS),
                         reads=[PB[7]], writes=[srtB])
                    S.op("dve", lambda e: e.reciprocal(out=srs[:], in_=srt[:]), reads=[srtB], writes=[srsB])
                    for i in range(8):
                        S.op("dve", lambda e, i=i: e.scalar_tensor_tensor(out=mix[:, 8 + i, cs], in0=yg[:, i, :],
                                                                          scalar=pkc("gssd", i), in1=srs[:], op0=ALU.mult, op1=ALU.mult),
                             reads=[ygB, srsB, cB], writes=[mixB])
                    S.op("dve", lambda e: e.tensor_tensor(out=Fst[:].rearrange("p (h d) -> p h d", h=16),
                                                          in0=Fst[:].rearrange("p (h d) -> p h d", h=16),
                                                          in1=bc(dec[:, c, :], 64), op=ALU.mult), reads=[FstB, decB, SbfB], writes=[FstB])
                    S.op("dve", lambda e: e.tensor_tensor(out=Fst[:], in0=Fst[:], in1=Sloc[:, c, :], op=ALU.add),
                         reads=[FstB, SlocB[c]], writes=[FstB])
                if stop_after == "ssd":
                    dump("acs", acs_t[:, 0:8, :], [128, 8, 16], F32, [acsB])
                    dump("dec", dec[:], [128, 9, 16], F32, [decB])
                    dump("tot", tot_sb[:], [128, 9, 16], F32, [totB])
                    dump("Sloc", Sloc[:], [128, 9, 1024], BF16, SlocB)
                    dump("Send", Fst[:], [128, 1024], F32, [FstB])
                    dump("Lx", Lx[:], [128, 1024], F32, [LxB])
                    dump("Ex", Ex[:], [128, 1024], F32, [ExB])
                    dump("CBm", CBm[:], [128, 128], F32, [CBmB])
                    dump("MT", MT[:], [128, 8, 128], BF16, [MTB])
                    dump("Cs", Cs[:], [128, 8, 128], BF16, [CsB])
                    dump("yg", yg[:], [128, 8, 128], F32, [ygB])
                    dump("Xdt", Xdt[:], [128, 8, 1024], BF16, XdtB)
                    dump("Sbf", Sbf[:], [128, 1024], BF16, [SbfB])
                    dump("coef", coef[:], [128, 5, 16], F32, [coefB])
                S.barrier()
            if stop_after == "ssd":
                dump("mix", mix[:, 8:16, :], [128, 8, NR], BF16, [mixB])
                dump("dt", dt_sb[:, 0:8, :], [128, 8, 16], F32, [dtB])
                dump("dA", dA_sb[:, 0:8, :], [128, 8, 16], F32, [dAB])
                dump("xtok", xtok_d[0:8], [8, 128, 1024], F32, xtokB)
                dump("btok", btok_d[0:8], [8, 128, 256], BF16, btokB)
                dump("bfm", bfm_d, [2, 128, NR + 16], BF16, bfmB)
                dump("cfm", cfm_d[:, :, 0:NR], [2, 128, NR], BF16, cfmB)
                dump("xfm", xfm_d, [8, 128, NR], F32, xfmB)
                dump("zs", zs_d, [8, 128, NR], BF16, zsB)
                finish()
                return nc
            with contextlib.ExitStack() as ast_:
                def asb(name, shape, dt=F32):
                    return ast_.enter_context(nc.sbuf_tensor("a_" + name, list(shape), dt))
                Kb = [[asb(f"K{sl}{s}", [128, 4 * NR], BF16) for s in range(2)] for sl in range(2)]
                KB = [[Buf(f"K{sl}{s}") for s in range(2)] for sl in range(2)]
                Qb = [[asb(f"Q{sl}{s}", [128, NR], BF16) for s in range(2)] for sl in range(2)]
                QB = [[Buf(f"Q{sl}{s}") for s in range(2)] for sl in range(2)]
                Vb = [asb(f"V{sl}", [128, 32, 128], BF16) for sl in range(2)]
                VB = [Buf(f"V{sl}") for sl in range(2)]
                NE = 6
                Eb = [asb(f"E{i}", [128, 512], BF16) for i in range(NE)]
                EB = [Buf(f"E{i}") for i in range(NE)]
                r0 = asb("r0", [128, 512]); r0B = Buf("r0")
                r1 = asb("r1", [128, 512]); r1B = Buf("r1")
                o0 = asb("o0", [128, 512]); o0B = Buf("o0")
                o1 = asb("o1", [128, 512]); o1B = Buf("o1")
                osq = asb("osq", [128, 512], BF16); osqB = Buf("osq")
                art = asb("art", [128, 512]); artB = Buf("art")
                ars = asb("ars", [128, 512]); arsB = Buf("ars")
                kview = [kall[j].rearrange("(r n) t -> n r t", r=4) for j in range(4)]
                for sl in range(2):
                    S.dma("sp", Kb[sl][0][64:128, :], umask_d[:, :], writes=[KB[sl][0]])
                    S.dma("sp", Kb[sl][1][0:64, :], umask_d[:, :], writes=[KB[sl][1]])
                    S.dma("sp", Qb[sl][0][64:128, :], wmask_d[:, :], writes=[QB[sl][0]])
                    S.dma("sp", Qb[sl][1][0:64, :], wmask_d[:, :], writes=[QB[sl][1]])
                for hd in range(8):
                    sl = hd % 2
                    for j in range(4):
                        S.dma("sp", Kb[sl][0][0:64, j * 1024:(j + 1) * 1024].rearrange("p (r t) -> p r t", r=4),
                              kview[j][hd * 128:hd * 128 + 64], reads=[kaB[j]], writes=[KB[sl][0]])
                        S.dma("sp", Kb[sl][1][64:128, j * 1024:(j + 1) * 1024].rearrange("p (r t) -> p r t", r=4),
                              kview[j][hd * 128 + 64:hd * 128 + 128], reads=[kaB[j]], writes=[KB[sl][1]])
                        S.dma("act", Vb[sl][:, j * 8:(j + 1) * 8, :],
                              vall[j].rearrange("(b p) f -> p b f", p=128)[:, :, hd * 128:(hd + 1) * 128],
                              reads=[vaB[j]], writes=[VB[sl]])
                    S.dma("act", Qb[sl][0][0:64, :], q_d[hd, 0:64, :], reads=[q_dB[hd]], writes=[QB[sl][0]])
                    S.dma("act", Qb[sl][1][64:128, :], q_d[hd, 64:128, :], reads=[q_dB[hd]], writes=[QB[sl][1]])
                    for qb in range(2):
                        qs = slice(qb * 512, (qb + 1) * 512)
                        units = [(kb, s) for kb in range(33) for s in range(2)]

                        def opnds(kb, s):
                            if kb < 32:
                                return (Kb[sl][s][:, kb * 128:(kb + 1) * 128], [KB[sl][s]], Vb[sl][:, kb, :], [VB[sl]], 128)
                            return (kmeta[:, hd, s, :], [kmB], vmeta[0:16, hd * 128:(hd + 1) * 128], [vmB], 16)

                        def emit_S(i):
                            kb, s = units[i]
                            lk, rdk, lv, rdv, nk = opnds(kb, s)
                            bs = 4 + i % 4
                            S.op("pe", lambda e: e.matmul(PS[bs][0:nk, :], lk, Qb[sl][s][:, qs], start=True, stop=True),
                                 reads=rdk + [QB[sl][s]], writes=[PB[bs]])

                        def emit_E(i):
                            kb, s = units[i]
                            nk = 128 if kb < 32 else 16
                            bs = 4 + i % 4
                            S.op("act", lambda e: e.activation(out=Eb[i % NE][0:nk, :], in_=PS[bs][0:nk, :], func=AF.Exp, scale=0.125),
                                 reads=[PB[bs]], writes=[EB[i % NE]])

                        def emit_PV(i):
                            kb, s = units[i]
                            lk, rdk, lv, rdv, nk = opnds(kb, s)
                            S.op("pe", lambda e: e.matmul(PS[s][:, :], lv, Eb[i % NE][0:nk, :], start=(kb == 0), stop=(kb == 32)),
                                 reads=rdv + [EB[i % NE]], writes=[PB[s]])
                            S.op("pe", lambda e: e.matmul(PS[2 + s][:, :], ones_b[0:nk, :], Eb[i % NE][0:nk, :], start=(kb == 0), stop=(kb == 32)),
                                 reads=[cB, EB[i % NE]], writes=[PB[2 + s]])

                        emit_S(0)
                        emit_S(1)
                        emit_S(2)
                        for i in range(len(units)):
                            emit_E(i)
                            if i + 3 < len(units):
                                emit_S(i + 3)
                            emit_PV(i)
                        S.op("dve", lambda e: e.tensor_copy(out=o0[:], in_=PS[0][:, :]), reads=[PB[0]], writes=[o0B])
                        S.op("dve", lambda e: e.tensor_copy(out=o1[:], in_=PS[1][:, :]), reads=[PB[1]], writes=[o1B])
                        S.op("dve", lambda e: e.tensor_copy(out=r0[:], in_=PS[2][:, :]), reads=[PB[2]], writes=[r0B])
                        S.op("dve", lambda e: e.tensor_copy(out=r1[:], in_=PS[3][:, :]), reads=[PB[3]], writes=[r1B])
                        S.op("dve", lambda e: e.reciprocal(out=r0[:], in_=r0[:]), reads=[r0B], writes=[r0B])
                        S.op("dve", lambda e: e.reciprocal(out=r1[:], in_=r1[:]), reads=[r1B], writes=[r1B])
                        S.op("dve", lambda e: e.tensor_tensor(out=o0[:], in0=o0[:], in1=r0[:], op=ALU.mult),
                             reads=[o0B, r0B], writes=[o0B])
                        S.op("dve", lambda e: e.tensor_tensor(out=o1[:], in0=o1[:], in1=r1[:], op=ALU.mult),
                             reads=[o1B, r1B], writes=[o1B])
                        S.op("dve", lambda e: e.scalar_tensor_tensor(out=o0[:], in0=o1[:], scalar=nlam[:, 0:1], in1=o0[:],
                                                                     op0=ALU.mult, op1=ALU.add),
                             reads=[o1B, o0B, nlamB], writes=[o0B])
                        S.op("act", lambda e: e.activation(out=osq[:], in_=o0[:], func=AF.Square), reads=[o0B], writes=[osqB])
                        S.op("pe", lambda e: e.matmul(PS[7][:, :], ones_b, osq[:], start=True, stop=True),
                             reads=[osqB, cB], writes=[PB[7]])
                        S.op("act", lambda e: e.activation(out=art[:], in_=PS[7][:, :], func=AF.Sqrt,
                                                           scale=1.0 / (128 * 0.64), bias=EPS / 0.64), reads=[PB[7]], writes=[artB])
                        S.op("dve", lambda e: e.reciprocal(out=ars[:], in_=art[:]), reads=[artB], writes=[arsB])
                        S.op("dve", lambda e: e.scalar_tensor_tensor(out=mix[:, hd, qs], in0=o0[:], scalar=pkc("gattn"), in1=ars[:],
                                                                     op0=ALU.mult, op1=ALU.mult),
                             reads=[o0B, arsB, cB], writes=[mixB])
                S.barrier()
            if stop_after == "attn":
                dump("mix", mix[:], [128, DC, NR], BF16, [mixB])
                finish()
                return nc
            with contextlib.ExitStack() as ost:
                wob = [ost.enter_context(nc.sbuf_tensor(f"o_wob{i}", [128, DC, 128], BF16)) for i in range(2)]
                woB = [Buf(f"wob{i}") for i in range(2)]
                io = 0
                for dc in range(DC):
                    sl = dc % 2
                    S.dma("pool", wob[sl][:], wout[dc], writes=[woB[sl]])
                    for (t0, tn) in ((0, 342), (342, 342), (684, 340)):
                        b = io % 4; io += 1
                        for mc in range(DC):
                            S.op("pe", lambda e, mc=mc: e.matmul(PS[b][:, 0:tn], wob[sl][:, mc, :], mix[:, mc, t0:t0 + tn],
                                                                 start=(mc == 0), stop=(mc == DC - 1)),
                                 reads=[woB[sl], mixB], writes=[PB[b]])
                        S.op("dve", lambda e: e.tensor_tensor(out=h[:, dc, t0:t0 + tn], in0=PS[b][:, 0:tn], in1=h[:, dc, t0:t0 + tn], op=ALU.add),
                             reads=[PB[b], hB], writes=[hB])
                S.barrier()
            if stop_after == "outproj":
                dump("h", h[:], [128, DC, NT], F32, [hB])
                finish()
                return nc
        ffn(1, "g2", [(0, 342), (342, 342), (684, 340)])
        finish()
    return nc


def _prep(inputs):
    f32 = np.float32
    x = np.asarray(inputs["x"], f32)
    meta = np.asarray(inputs["meta_tokens"], f32)

    def gate_tiles(W):
        return np.ascontiguousarray(W.reshape(DC, 128, FC, 128).transpose(2, 1, 0, 3))

    def down_tiles(W):
        return np.ascontiguousarray(W.reshape(NQR, FQ, 128, DC, 128).transpose(0, 3, 2, 1, 4))

    shared = {}
    shared["wg1"] = gate_tiles(inputs["ffn1_w_gate"][0])
    shared["wu1"] = gate_tiles(inputs["ffn1_w_up"][0])
    shared["wd1"] = down_tiles(inputs["ffn1_w_down"][0])
    shared["wg2"] = gate_tiles(inputs["ffn2_w_gate"][0])
    shared["wu2"] = gate_tiles(inputs["ffn2_w_up"][0])
    shared["wd2"] = down_tiles(inputs["ffn2_w_down"][0])
    Win = np.asarray(inputs["w_in"][0], f32)
    cols = np.r_[0:2048, 3072:4096, 4096:5632]
    shared["winfm"] = np.ascontiguousarray(Win[:, cols].reshape(DC, 128, 36, 128).transpose(2, 1, 0, 3))
    shared["winv"] = np.ascontiguousarray(Win[:, 2048:3072].reshape(DC, 128, 1024).transpose(1, 0, 2))
    shared["windt"] = np.ascontiguousarray(Win[:, 5632:5648].reshape(DC, 128, 16).transpose(1, 0, 2))
    Wout = np.asarray(inputs["w_out"][0], f32)
    shared["wout"] = np.ascontiguousarray(Wout.reshape(DC, 128, DC, 128).transpose(2, 1, 0, 3))

    def fm(v, n):
        return np.asarray(v, f32).reshape(n, 128).T

    pkb = np.zeros((128, NPK), f32)

    def put(name, arr):
        o, w = PK[name]
        pkb[:, o:o + w] = arr

    put("g1", fm(inputs["ffn1_norm"][0], 16))
    put("gm", fm(inputs["mix_norm"][0], 16))
    put("g2", fm(inputs["ffn2_norm"][0], 16))
    put("gssd", fm(inputs["ssd_norm"][0], 8))
    put("gattn", np.asarray(inputs["attn_out_norm"][0], f32).reshape(128, 1))
    put("gq", np.tile(np.asarray(inputs["q_norm"][0], f32), 2).reshape(128, 1))
    put("gk", np.tile(np.asarray(inputs["k_norm"][0], f32), 2).reshape(128, 1))
    cw = np.asarray(inputs["conv_w"][0], f32)
    put("cw", cw.reshape(4, 12, 128).transpose(2, 1, 0).reshape(128, 48))
    put("cb", fm(inputs["conv_b"][0], 12))
    dsk = np.asarray(inputs["d_skip"][0], f32)
    put("dsk", np.repeat(dsk.reshape(8, 2), 64, axis=1).T)
    put("dtb", np.tile(np.asarray(inputs["dt_bias"][0], f32)[None], (128, 1)))
    put("alog", np.tile(np.asarray(inputs["a_log"][0], f32)[None], (128, 1)))
    lamc = np.zeros((128, 4), f32)
    for i, n in enumerate(["lambda_q1", "lambda_k1", "lambda_q2", "lambda_k2"]):
        lamc[:64, i] = np.asarray(inputs[n][0], f32)
    put("lam", lamc)

    ident = np.eye(128, dtype=f32)
    ones = np.ones((128, 128), f32)
    bd = np.zeros((128, 128), f32); bd[:64, :64] = 1; bd[64:, 64:] = 1
    rot = np.zeros((128, 128), f32)
    for blk in (0, 64):
        for d in range(8):
            rot[blk + d + 8, blk + d] = -1.0
            rot[blk + d, blk + d + 8] = 1.0
    shared["cbf"] = np.ascontiguousarray(np.stack([ident, ones, bd, rot], axis=1)).astype(NPBF)
    jj = np.arange(128)
    tle = (jj[:, None] <= jj[None, :]).astype(f32)
    tgt = (jj[:, None] > jj[None, :]).astype(f32)
    shared["cf"] = np.ascontiguousarray(np.stack([ident, ones, tle, tgt], axis=1))
    kk = np.arange(4 * NR)
    gtok = ((kk % 1024) // 256) * 1024 + (kk // 1024) * 256 + kk % 256
    shared["umask"] = (gtok[None, :] // 64 == np.arange(64)[:, None]).astype(f32).astype(NPBF)

    inv = np.power(500000.0, -np.arange(0, 16, 2, dtype=f32) / 16).astype(f32)
    in_maps = []
    for r in range(8):
        b, q = r // 4, r % 4
        real = x[b, q * NR:(q + 1) * NR]
        halo = meta[13:16] if q == 0 else x[b, q * NR - 3:q * NR]
        tok = np.concatenate([real, meta, halo, np.zeros((NT - HALO0 - 3, D), f32)], axis=0)
        m = dict(shared)
        m["xT"] = np.ascontiguousarray(tok.T.reshape(DC, 128, NT).transpose(1, 0, 2))
        pos = np.zeros(NT, f32)
        pos[:NR] = 16 + q * NR + np.arange(NR)
        pos[META0:META0 + 16] = np.arange(16)
        ang = pos[None, :] * inv[:, None]
        cosF = np.ones((128, NT), f32); sinF = np.zeros((128, NT), f32)
        for blk in (0, 64):
            cosF[blk:blk + 8] = np.cos(ang); cosF[blk + 8:blk + 16] = np.cos(ang)
            sinF[blk:blk + 8] = np.sin(ang); sinF[blk + 8:blk + 16] = np.sin(ang)
        m["cosF"] = cosF; m["sinF"] = sinF
        tq = q * 16 + np.arange(NR) // 64
        m["wmask"] = np.where(tq[None, :] < np.arange(64)[:, None], NEG, 0.0).astype(f32).astype(NPBF)
        pkr = pkb.copy()
        bm = np.zeros((4, 5), f32)
        for q2 in range(4):
            for q1 in range(4):
                bm[q2, q1] = 1.0 if (q1 < q2 < q) else 0.0
            bm[q2, 4] = 1.0 if q2 < q else 0.0
        o, w = PK["bm"]; pkr[:, o:o + w] = bm.reshape(1, 20)
        msk = np.array([1.0 if q1 < q else 0.0 for q1 in range(4)] + [1.0], f32)
        o, w = PK["msk"]; pkr[:, o:o + w] = msk[None]
        m["pk"] = pkr
        in_maps.append(m)
    return in_maps


_NC_CACHE = {}


def kernel(**inputs):
    in_maps = _prep(inputs)
    key = "full"
    if key not in _NC_CACHE:
        _NC_CACHE[key] = build_program()
    nc = _NC_CACHE[key]
    res = run_bass_kernel_spmd(nc, in_maps, core_ids=list(range(8)))
    out = np.zeros((2, 4096, D), np.float32)
    for r in range(8):
        b, q = r // 4, r % 4
        o = res.results[r]["outT"]
        out[b, q * NR:(q + 1) * NR] = o.transpose(2, 1, 0).reshape(NR, D)
    return out
```
